# Optimizing a Trainium2 kernel written in Bass

```python
import jax, jax.numpy as jnp
from jax import lax
import numpy as np

D_MODEL = 1024
BATCH = 8
SEQ = 2048
DEPTH = 2
DEC_BATCH = 128
DEC_SEQ = 8
PAST_LEN = 16384
PAGE_SIZE = 128

D_MIX = D_MODEL
D_A = D_MIX // 2
A_HEADS = 8
A_HEAD_DIM = D_A // A_HEADS
CHUNK = 128
D_B = D_MIX // 4
CONV_B = 31
D_C = D_MIX - D_A - D_B
CONV_C = 3
D_IN = 2 * D_A + 2 * D_B + 3 * D_C
D_FF = 4 * D_MODEL
EPS = 1e-6

kernel_name = "hybrid_chunkmlp_conformerconv_shortconv_step"


def rmsnorm(x, g):
    xf = x.astype(jnp.float32)
    y = xf * lax.rsqrt(jnp.mean(xf * xf, axis=-1, keepdims=True) + EPS)
    return (y * g.astype(jnp.float32)).astype(x.dtype)


def layernorm(x, g, b):
    xf = x.astype(jnp.float32)
    mu = jnp.mean(xf, axis=-1, keepdims=True)
    var = jnp.mean(jnp.square(xf - mu), axis=-1, keepdims=True)
    y = (xf - mu) * lax.rsqrt(var + EPS)
    return (y * g.astype(jnp.float32) + b.astype(jnp.float32)).astype(x.dtype)


def causal_dwconv(xpad, w):
    return lax.conv_general_dilated(
        xpad, w[:, None, :].astype(xpad.dtype), window_strides=(1,), padding='VALID',
        dimension_numbers=('NWC', 'WIO', 'NWC'), feature_group_count=xpad.shape[-1])


def spatial_prompt(v, ws_tril, bias):
    n, t, _ = v.shape
    vc = v.reshape(n, t // CHUNK, CHUNK, A_HEADS, A_HEAD_DIM)
    mix = jnp.einsum('hts,ncshd->ncthd', ws_tril, vc) + bias.T[None, None, :, :, None]
    return mix.reshape(n, t, D_A)


def spatial_sample(v, ws_tril, bias):
    n, t, _ = v.shape
    p0 = PAST_LEN % CHUNK
    w = ws_tril[:, p0:p0 + t, p0:p0 + t]
    bb = bias[:, p0:p0 + t]
    vh = v.reshape(n, t, A_HEADS, A_HEAD_DIM)
    mix = jnp.einsum('hts,nshd->nthd', w, vh) + bb.T[None, :, :, None]
    return mix.reshape(n, t, D_A)


def mixer_block(h, hist_b, hist_c, spatial_fn, w_in, a_ln_g, a_ln_b, ws_tril, a_bias,
                b_conv_w, b_conv_b, b_ln_g, b_ln_b, c_conv_w, w_out):
    z = h @ w_in
    s = np.cumsum([D_A, D_A, D_B, D_B, D_C, D_C])
    za_u, za_v, zb_a, zb_g, zc_b, zc_c, zc_x = jnp.split(z, list(s), axis=-1)
    u = jax.nn.gelu(za_u)
    v = layernorm(jax.nn.gelu(za_v), a_ln_g, a_ln_b)
    ya = u * spatial_fn(v, ws_tril, a_bias)
    glu = zb_a * jax.nn.sigmoid(zb_g)
    xb = jnp.concatenate([hist_b, glu], axis=1)
    yb = causal_dwconv(xb, b_conv_w) + b_conv_b
    yb = jax.nn.silu(layernorm(yb, b_ln_g, b_ln_b))
    cx = zc_c * zc_x
    xc = jnp.concatenate([hist_c, cx], axis=1)
    yc = zc_b * causal_dwconv(xc, c_conv_w)
    y = jnp.concatenate([ya, yb, yc], axis=-1) @ w_out
    return (y, xb[:, xb.shape[1] - (CONV_B - 1):], xc[:, xc.shape[1] - (CONV_C - 1):], v)


def setup_inputs(seed: int = 0) -> dict:
    key = jax.random.key(seed)
    ks = jax.random.split(key, 24)
    nrm = lambda k, shp, sc: jax.random.normal(k, shp, jnp.float32) * sc
    return {
        "x_prompt": nrm(ks[0], (BATCH, SEQ, D_MODEL), 1.0),
        "x_sample": nrm(ks[1], (DEC_BATCH, DEC_SEQ, D_MODEL), 1.0),
        "state_conv_b": nrm(ks[2], (DEPTH, DEC_BATCH, CONV_B - 1, D_B), 0.5),
        "state_conv_c": nrm(ks[3], (DEPTH, DEC_BATCH, CONV_C - 1, D_C), 0.5),
        "norm1_g": 1.0 + nrm(ks[4], (DEPTH, D_MODEL), 0.02),
        "w_in": nrm(ks[5], (DEPTH, D_MODEL, D_IN), D_MODEL ** -0.5),
        "a_ln_g": 1.0 + nrm(ks[6], (DEPTH, D_A), 0.02),
        "a_ln_b": nrm(ks[7], (DEPTH, D_A), 0.02),
        "a_ws": nrm(ks[8], (DEPTH, A_HEADS, CHUNK, CHUNK), CHUNK ** -0.5),
        "a_bias": 1.0 + nrm(ks[9], (DEPTH, A_HEADS, CHUNK), 0.02),
        "b_conv_w": nrm(ks[10], (DEPTH, CONV_B, D_B), CONV_B ** -0.5),
        "b_conv_b": nrm(ks[11], (DEPTH, D_B), 0.02),
        "b_ln_g": 1.0 + nrm(ks[12], (DEPTH, D_B), 0.02),
        "b_ln_b": nrm(ks[13], (DEPTH, D_B), 0.02),
        "c_conv_w": nrm(ks[14], (DEPTH, CONV_C, D_C), CONV_C ** -0.5),
        "w_out": nrm(ks[15], (DEPTH, D_MIX, D_MODEL), D_MIX ** -0.5),
        "norm2_g": 1.0 + nrm(ks[16], (DEPTH, D_MODEL), 0.02),
        "w_ff1": nrm(ks[17], (DEPTH, D_MODEL, D_FF), D_MODEL ** -0.5),
        "w_ff2": nrm(ks[18], (DEPTH, D_FF, D_MODEL), D_FF ** -0.5),
        "norm_f_g": 1.0 + nrm(ks[19], (D_MODEL,), 0.02),
    }


def reference(x_prompt, x_sample, state_conv_b, state_conv_c, norm1_g, w_in, a_ln_g, a_ln_b,
              a_ws, a_bias, b_conv_w, b_conv_b, b_ln_g, b_ln_b, c_conv_w, w_out, norm2_g,
              w_ff1, w_ff2, norm_f_g):
    xp, xs = x_prompt, x_sample
    cb_p, cc_p, cb_s, cc_s, v_s = [], [], [], [], []
    for l in range(DEPTH):
        ws_tril = jnp.tril(a_ws[l])
        lw = (w_in[l], a_ln_g[l], a_ln_b[l], ws_tril, a_bias[l], b_conv_w[l], b_conv_b[l],
              b_ln_g[l], b_ln_b[l], c_conv_w[l], w_out[l])
        hp = rmsnorm(xp, norm1_g[l])
        zb = jnp.zeros((xp.shape[0], CONV_B - 1, D_B), xp.dtype)
        zc = jnp.zeros((xp.shape[0], CONV_C - 1, D_C), xp.dtype)
        yp, nbp, ncp, _ = mixer_block(hp, zb, zc, spatial_prompt, *lw)
        xp = xp + yp
        hp = rmsnorm(xp, norm2_g[l])
        xp = xp + jnp.square(jax.nn.relu(hp @ w_ff1[l])) @ w_ff2[l]
        hs = rmsnorm(xs, norm1_g[l])
        ys, nbs, ncs, vs = mixer_block(hs, state_conv_b[l], state_conv_c[l], spatial_sample, *lw)
        xs = xs + ys
        hs = rmsnorm(xs, norm2_g[l])
        xs = xs + jnp.square(jax.nn.relu(hs @ w_ff1[l])) @ w_ff2[l]
        cb_p.append(nbp); cc_p.append(ncp); cb_s.append(nbs); cc_s.append(ncs); v_s.append(vs)
    y_prompt = rmsnorm(xp, norm_f_g)
    y_sample = rmsnorm(xs, norm_f_g)
    new_conv_b_prompt = jnp.stack(cb_p)
    new_conv_c_prompt = jnp.stack(cc_p)
    new_conv_b_sample = jnp.stack(cb_s)
    new_conv_c_sample = jnp.stack(cc_s)
    new_chunk_v_sample = jnp.stack(v_s)
    return (y_prompt, y_sample, new_conv_b_prompt, new_conv_c_prompt, new_conv_b_sample, new_conv_c_sample, new_chunk_v_sample)
```

```python
import numpy as np
from contextlib import ExitStack
import concourse.bass as bass
import concourse.mybir as mybir
from concourse.bass_utils import run_bass_kernel_spmd

F32 = mybir.dt.float32
BF16 = mybir.dt.bfloat16
AF = mybir.ActivationFunctionType
ALU = mybir.AluOpType

D = 1024
DA = 512
DB = 256
DC = 256
DIN = 2304
DFF = 4096
L = 2
TP = 2048
TS = 128
T = TP + TS
NQ = 16
EPS = 1e-6
NCORES = 8
RELAX_SAME_ENGINE_RAW = False
NPE_CONV = 14
ENG_LNAFF = "pool"
ENG_CONVC = "pool"


class Rec:
    def __init__(self):
        self.calls = []

    def __getattr__(self, name):
        def m(*a, **k):
            self.calls.append((name, a, k))
            return self
        return m


class FW:
    def __init__(self, nc, es, ndma=28):
        self.nc = nc
        hs = dict(pe=nc.tensor, act=nc.scalar, dve=nc.vector, pool=nc.gpsimd, sp=nc.sync)
        self.E = {}
        for k, h in hs.items():
            self.E[k] = dict(h=h, sem=es.enter_context(nc.semaphore("s_" + k)), cnt=0, seen={}, prog=[])
        self.dsem = [es.enter_context(nc.semaphore("dq%d" % i)) for i in range(ndma)]
        self.dval = [0] * ndma
        self.dpool = {"sp": list(range(0, 18)), "pool": list(range(18, ndma))}
        self.dcur = {"sp": 0, "pool": 0}
        self.lastw = {}
        self.readers = {}
        self.bank = 0
        self.banktime = {}
        self.held = set()
        self.opidx = 0

    def _need(self, en, reads, writes):
        e = self.E[en]
        need = {}

        def add(dep, raw):
            src, sem, val = dep
            if src == en:
                if en in ("pe", "sp") or not raw:
                    return
                if RELAX_SAME_ENGINE_RAW and en in ("dve", "act") and val < e["cnt"]:
                    return
            if need.get(src, (None, 0))[1] < val:
                need[src] = (sem, val)

        for k in reads:
            if k in self.lastw:
                add(self.lastw[k], True)
        for k in writes:
            if k in self.lastw:
                add(self.lastw[k], False)
            for d in self.readers.get(k, {}).values():
                add(d, False)
        for src, (sem, val) in need.items():
            if e["seen"].get(src, 0) < val:
                e["seen"][src] = val
                e["prog"].append(("w", sem, val))

    def _record(self, dep, reads, writes):
        self.opidx += 1
        for k in list(reads) + list(writes):
            if isinstance(k, tuple) and k[0] == "ps":
                self.banktime[k[1]] = self.opidx
        for k in writes:
            self.lastw[k] = dep
            self.readers[k] = {}
        for k in reads:
            r = self.readers.setdefault(k, {})
            if r.get(dep[0], (None, None, 0))[2] < dep[2]:
                r[dep[0]] = dep

    def op(self, en, fn, reads=(), writes=()):
        e = self.E[en]
        self._need(en, reads, writes)
        e["cnt"] += 1
        r = Rec()
        fn(r)
        assert r.calls
        e["prog"].append(("i", r.calls))
        self._record((en, e["sem"], e["cnt"]), reads, writes)

    def dma(self, q, out, in_, reads=(), writes=(), **kw):
        e = self.E[q]
        slots = self.dpool[q]
        i = slots[self.dcur[q] % len(slots)]
        self.dcur[q] += 1
        sem = self.dsem[i]
        if self.dval[i] > 0 and e["seen"].get(("dma", i), 0) < self.dval[i]:
            e["seen"][("dma", i)] = self.dval[i]
            e["prog"].append(("w", sem, self.dval[i]))
        self._need(q, reads, writes)
        self.dval[i] += 16
        e["prog"].append(("d", out, in_, sem, kw))
        self._record((("dma", i), sem, self.dval[i]), reads, writes)

    def barrier(self, only=None, keep_pool_dma=False):
        pool_slots = set(self.dpool["pool"]) if keep_pool_dma else set()
        kept = {k: d for k, d in self.lastw.items()
                if isinstance(d[0], tuple) and d[0][0] == "dma" and d[0][1] in pool_slots}
        for en, e in self.E.items():
            if only is not None and en not in only:
                continue
            for en2, e2 in self.E.items():
                if en2 == en and en in ("pe", "sp"):
                    continue
                if e2["cnt"] > e["seen"].get(en2, 0):
                    e["seen"][en2] = e2["cnt"]
                    e["prog"].append(("w", e2["sem"], e2["cnt"]))
            for i, v in enumerate(self.dval):
                if i in pool_slots:
                    continue
                if v > e["seen"].get(("dma", i), 0):
                    e["seen"][("dma", i)] = v
                    e["prog"].append(("w", self.dsem[i], v))
        if only is None:
            self.lastw.clear()
            self.readers.clear()
            self.lastw.update(kept)

    def nextbank(self, hold=False):
        best, bi = None, None
        for b in range(8):
            if b in self.held:
                continue
            t = self.banktime.get(b, -1)
            if best is None or t < best:
                best, bi = t, b
        assert bi is not None, "no free PSUM bank"
        self.banktime[bi] = self.opidx + 0.5
        if hold:
            self.held.add(bi)
        return bi

    def release(self, b):
        self.held.discard(b)

    def emit(self):
        with self.nc.Block() as block:
            def mk(en):
                e = self.E[en]

                def f(h):
                    for it in e["prog"]:
                        if it[0] == "w":
                            h.wait_ge(it[1], it[2])
                        elif it[0] == "i":
                            ins = None
                            for (nm, a, k) in it[1]:
                                ins = getattr(h, nm)(*a, **k)
                            ins.then_inc(e["sem"], 1)
                        else:
                            h.dma_start(out=it[1], in_=it[2], **it[4]).then_inc(it[3], 16)
                return f
            block.tensor(mk("pe"))
            block.scalar(mk("act"))
            block.vector(mk("dve"))
            block.gpsimd(mk("pool"))
            block.sync(mk("sp"))


def xkeys(t0, n):
    return [("x", i) for i in range(t0 // 128, (t0 + n + 127) // 128)]


def build_nc(dbg=False):
    nc = bass.Bass("TRN2", target_bir_lowering=False)

    def din(name, shape):
        return nc.dram_tensor(name, list(shape), F32, kind="ExternalInput").ap()

    def dout(name, shape):
        return nc.dram_tensor(name, list(shape), F32, kind="ExternalOutput").ap()

    xin = din("xin", [T, D])
    scb = din("scb", [L, NQ * 30, DB])
    scc = din("scc", [L, NQ * 2, DC])
    norm1_g = din("norm1_g", [L, D])
    w_in = din("w_in", [L, D, DIN])
    a_ln_g = din("a_ln_g", [L, DA])
    a_ln_b = din("a_ln_b", [L, DA])
    a_ws = din("a_ws", [L, 8, 128, 128])
    a_bias = din("a_bias", [L, 8, 128])
    b_conv_w = din("b_conv_w", [L, 31, DB])
    b_conv_b = din("b_conv_b", [L, DB])
    b_ln_g = din("b_ln_g", [L, DB])
    b_ln_b = din("b_ln_b", [L, DB])
    c_conv_w = din("c_conv_w", [L, 3, DC])
    w_out = din("w_out", [L, D, D])
    norm2_g = din("norm2_g", [L, D])
    w_ff1 = din("w_ff1", [L, D, DFF])
    w_ff2 = din("w_ff2", [L, DFF, D])
    norm_f_g = din("norm_f_g", [D])

    y_o = dout("y", [T, D])
    ncbp_o = dout("ncb_p", [L, 30, DB])
    nccp_o = dout("ncc_p", [L, 2, DC])
    ncbs_o = dout("ncb_s", [L, NQ * 30, DB])
    nccs_o = dout("ncc_s", [L, NQ * 2, DC])
    vs_o = dout("v_s", [L, TS, DA])

    with ExitStack() as es:
        fw = FW(nc, es)
        op, dma = fw.op, fw.dma

        _uid = [0]

        def sb(stack, name, shape, dt):
            _uid[0] += 1
            return stack.enter_context(nc.sbuf_tensor("%s_%d" % (name, _uid[0]), list(shape), dt))

        xT = sb(es, "xT", [128, 8, T], F32)
        RW = sb(es, "RW", [128, 32768], BF16)
        ident = sb(es, "ident", [128, 128], F32)
        ones_bf = sb(es, "ones_bf", [128, 128], BF16)
        identb = sb(es, "identb", [128, 128], BF16)
        ones32 = sb(es, "ones32", [128, 128], F32)
        mhalf = sb(es, "mhalf", [128, 512], F32)
        cA = sb(es, "cA", [128, 128], F32)
        cB = sb(es, "cB", [128, 64], F32)
        ps = [es.enter_context(nc.psum_tensor("ps%d" % i, [128, 512], F32)) for i in range(8)]

        def bcw(l, k, c):
            j = l * 62 + k * 2 + c
            return cA[:, j:j + 1]

        def bcb(l, c):
            j = 124 + l * 2 + c
            return cA[:, j:j + 1]

        def g1(l, c):
            j = l * 8 + c
            return cB[:, j:j + 1]

        def g2(l, c):
            j = 16 + l * 8 + c
            return cB[:, j:j + 1]

        def gf(l, c):
            j = 32 + c
            return cB[:, j:j + 1]

        def blg(l, c):
            j = 40 + l * 2 + c
            return cB[:, j:j + 1]

        def blb(l, c):
            j = 44 + l * 2 + c
            return cB[:, j:j + 1]

        def ccw(l, k, c):
            j = 48 + l * 6 + k * 2 + c
            return cB[:, j:j + 1]

        def mmgroup(b, n, lhs_fn, rhs_fn, nk, reads, f32=False, off=0):
            def f(h):
                ins = None
                for k in range(nk):
                    ins = h.matmul(ps[b][:, off:off + n], lhsT=lhs_fn(k), rhs=rhs_fn(k), start=(k == 0), stop=(k == nk - 1))
                return ins
            op("pe", f, reads=reads, writes=[("ps", b)])

        Win = RW[:, 0:8 * DIN].rearrange("p (k n) -> p k n", k=8)
        Wout = RW[:, 8 * DIN:8 * DIN + 8 * D].rearrange("p (k n) -> p k n", k=8)

        WCB = [(0, 512), (512, 1024), (1024, 1536), (1536, 2304)]

        def winkeys(cb):
            return [("win", k, cb) for k in range(8)]

        def load_mixer_weights(l, part="all", first_reads=()):
            k0 = 7 if l > 0 else 0
            if part in ("all", "win"):
                for k in range(k0, 8):
                    dma("pool", Win[:, k, 1536:2304], w_in[l, k * 128:(k + 1) * 128, 1536:2304],
                        reads=(list(first_reads) if k == k0 else []), writes=[("win", k, 3)])
                for k in range(k0, 8):
                    dma("pool", Win[:, k, 0:1536], w_in[l, k * 128:(k + 1) * 128, 0:1536],
                        writes=[("win", k, 0), ("win", k, 1), ("win", k, 2)])
            if part in ("all", "wout"):
                for k in range(8):
                    dma("pool", Wout[:, k, :], w_out[l, k * 128:(k + 1) * 128, :], writes=[("wout", k)])

        op("pool", lambda h: h.memset(ident[:], 0.0), writes=["ident"])
        op("pool", lambda h: h.affine_select(out=ident[:], in_=ident[:], pattern=[[-1, 128]],
                                             compare_op=ALU.not_equal, fill=1.0, base=0, channel_multiplier=1),
           reads=["ident"], writes=["ident"])
        op("pool", lambda h: h.memset(ones_bf[:], 1.0), writes=["ones_bf"])
        op("act", lambda h: h.activation(out=identb[:], in_=ident[:], func=AF.Copy), reads=["ident"], writes=["identb"])
        op("pool", lambda h: h.memset(ones32[:], 1.0), writes=["ones32"])
        op("pool", lambda h: h.memset(mhalf[:], -0.5), writes=["mhalf"])

        with ExitStack() as s0:
            NST = 6
            stage = [sb(s0, "xst%d" % i, [128, D], F32) for i in range(NST)]
            cstA = sb(s0, "cstA", [128, 128], F32)
            cstB = sb(s0, "cstB", [64, 128], F32)
            dma("sp", cstA[0:124, :], b_conv_w.rearrange("l k (c p) -> (l k c) p", p=128), writes=["cstA"])
            dma("sp", cstA[124:128, :], b_conv_b.rearrange("l (c p) -> (l c) p", p=128), writes=["cstA2"])
            dma("sp", cstB[0:16, :], norm1_g.rearrange("l (c p) -> (l c) p", p=128), writes=["cstB0"])
            dma("sp", cstB[16:32, :], norm2_g.rearrange("l (c p) -> (l c) p", p=128), writes=["cstB1"])
            dma("sp", cstB[32:40, :], norm_f_g.rearrange("(c p) -> c p", p=128), writes=["cstB2"])
            dma("sp", cstB[40:44, :], b_ln_g.rearrange("l (c p) -> (l c) p", p=128), writes=["cstB3"])
            dma("sp", cstB[44:48, :], b_ln_b.rearrange("l (c p) -> (l c) p", p=128), writes=["cstB4"])
            dma("sp", cstB[48:60, :], c_conv_w.rearrange("l k (c p) -> (l k c) p", p=128), writes=["cstB5"])
            b = fw.nextbank()
            op("pe", lambda h, b=b: h.transpose(ps[b][:, 0:128], cstA[:, :], ident[:]),
               reads=["cstA", "cstA2", "ident"], writes=[("ps", b)])
            op("dve", lambda h, b=b: h.tensor_copy(out=cA[:], in_=ps[b][:, 0:128]), writes=[("ps", b), "cA"])
            b = fw.nextbank()
            op("pe", lambda h, b=b: h.transpose(ps[b][:, 0:60], cstB[0:60, :], ident[0:60, 0:60]),
               reads=["cstB%d" % i for i in range(6)] + ["ident"], writes=[("ps", b)])
            op("dve", lambda h, b=b: h.tensor_copy(out=cB[:, 0:60], in_=ps[b][:, 0:60]), writes=[("ps", b), "cB"])

            for i in range(T // 128):
                st = stage[i % NST]
                dma("sp", st[:], xin[i * 128:(i + 1) * 128, :], writes=[("xst", i % NST)])
                if i == 9:
                    load_mixer_weights(0, part="win", first_reads=[("xst", i % NST)])
                if i == 12:
                    o0 = 8 * DIN + 8 * D
                    dma("sp", RW[:, o0:o0 + 2048].bitcast(F32).rearrange("p (h s) -> p h s", h=8), a_ws[0].rearrange("h t s -> t h s"),
                        writes=["Wm"])
                    dma("sp", RW[:, o0 + 3584:o0 + 4608].bitcast(F32), a_ln_g[0:1, :].partition_broadcast(128), writes=["alng"])
                    dma("sp", RW[:, o0 + 4608:o0 + 5632].bitcast(F32), a_ln_b[0:1, :].partition_broadcast(128), writes=["alnb"])
                for cg in range(2):
                    b = fw.nextbank()

                    def f(h, st=st, b=b, cg=cg):
                        ins = None
                        for j in range(4):
                            c = cg * 4 + j
                            ins = h.transpose(ps[b][:, j * 128:(j + 1) * 128], st[:, c * 128:(c + 1) * 128], ident[:])
                        return ins
                    op("pe", f, reads=[("xst", i % NST), "ident"], writes=[("ps", b)])
                    o_ap = xT[:, cg * 4:(cg + 1) * 4, i * 128:(i + 1) * 128]
                    i_ap = ps[b][:].rearrange("p (c t) -> p c t", c=4)
                    if cg == 0:
                        op("act", lambda h, o=o_ap, i_=i_ap: h.activation(out=o, in_=i_, func=AF.Copy),
                           reads=[], writes=[("ps", b), ("x", i)])
                    else:
                        op("dve", lambda h, o=o_ap, i_=i_ap: h.tensor_copy(out=o, in_=i_),
                           reads=[], writes=[("ps", b), ("x", i)])
            fw.barrier(keep_pool_dma=True)

        def norm(t0, n, gfn, hout_fn, hkey, sq, tA, tB, ksq, kA, kB):
            xk = xkeys(t0, n)
            op("act", lambda h: h.activation(out=sq[:, :, :n], in_=xT[:, :, t0:t0 + n], func=AF.Square),
               reads=xk, writes=[ksq])
            b = fw.nextbank()
            mmgroup(b, n, lambda k: ones_bf[:], lambda k: sq[:, k, :n], 8, [ksq, "ones_bf"])
            op("act", lambda h: h.activation(out=tA[:, :n], in_=ps[b][:, :n], func=AF.Ln, scale=1.0 / D, bias=EPS),
               writes=[("ps", b), kA])
            op("act", lambda h: h.activation(out=tB[:, :n], in_=tA[:, :n], func=AF.Exp, scale=-0.5), reads=[kA], writes=[kB])
            for c in range(8):
                op("dve", lambda h, c=c: h.scalar_tensor_tensor(out=hout_fn(c), in0=xT[:, c, t0:t0 + n], scalar=gfn(c),
                                                               in1=tB[:, :n], op0=ALU.mult, op1=ALU.mult),
                   reads=xk + [kB, "cB"], writes=[hkey])

        for l in range(L):
            W1 = [RW[:, i * 8192:(i + 1) * 8192].rearrange("p (k n) -> p k n", k=8) for i in range(2)]

            def load_w1(blk, extra=()):
                W = W1[blk % 2]
                for k in range(8):
                    dma("pool", W[:, k, :], w_ff1[l, k * 128:(k + 1) * 128, blk * 1024:(blk + 1) * 1024],
                        writes=[("w1", blk % 2, k)] + list(extra))

            with ExitStack() as sm:
                o = 8 * DIN + 8 * D
                Wm = RW[:, o:o + 2048].bitcast(F32).rearrange("p (h s) -> p h s", h=8)
                vf = RW[:, o + 2048:o + 3072].bitcast(F32)
                hst = RW[:, o + 3072:o + 3584].bitcast(F32)
                alng = RW[:, o + 3584:o + 4608].bitcast(F32)
                alnb = RW[:, o + 4608:o + 5632].bitcast(F32)
                if l > 0:
                    load_mixer_weights(l, part="win")
                WINK = [("win", k) for k in range(8)]
                WOUTK = [("wout", k) for k in range(8)]

                NM = 256
                NPOOL = 0
                hb1 = sb(sm, "hb", [128, 8, NM], BF16)
                hb = [hb1, hb1]
                GH = sb(sm, "GH", [128, 2, 30 + NM], BF16)
                GL = sb(sm, "GL", [128, 2, 30 + NM], BF16)
                dgs = sb(sm, "dgs", [128, max(1, 2 * NPE_CONV), 128], BF16)
                sq = sb(sm, "sq", [128, 8, NM], BF16)
                uT = sb(sm, "uT", [128, 4, NM], BF16)
                cat = [sb(sm, "cat%d" % i, [128, 8, NM], BF16) for i in range(2)]
                vE = [sb(sm, "vE%d" % i, [128, DA], BF16) for i in range(2)]
                vO = [sb(sm, "vO%d" % i, [128, DA], BF16) for i in range(2)]
                gvr = [sb(sm, "gv%d" % i, [128, DA], F32) for i in range(2)]
                sq32 = None
                NT = 7
                tr = [sb(sm, "tr%d" % i, [128, NM], F32) for i in range(NT)]
                vst = [sb(sm, "vst%d" % i, [128, 16], F32) for i in range(4)]
                G = [sb(sm, "G%d" % i, [128, 2, 30 + NM], F32) for i in range(2)]
                CX = [sb(sm, "CX%d" % i, [128, 2, 2 + NM], F32) for i in range(2)]
                GS = sb(sm, "GS", [128, 2, NQ, 38], F32)
                CXS = sb(sm, "CXS", [128, 2, NQ, 10], F32)
                acc = sb(sm, "acc", [128, 2, NM], F32)
                sqb = sb(sm, "sqb", [128, 2, NM], F32)
                wsT = sb(sm, "wsT", [128, 8, 128], BF16)
                BD = sb(sm, "BD", [128, 8, 128], BF16)
                biasR = sb(sm, "biasR", [2, 512], F32)
                biasRs = sb(sm, "biasRs", [2, 512], F32)
                Ind = sb(sm, "Ind", [2, 128], F32)
                accC = sb(sm, "accC", [128, 2, NM], F32)
                Sel = gvr[1][0:8, 0:128]
                Msk = gvr[0][:, 0:128]

                state = dict(tri=0, vsi=0)
                sq32 = sq[:, :, :].rearrange("p k n -> p (k n)").bitcast(F32)
                tr_held = set()

                def T1(hold=False):
                    for _ in range(NT):
                        i = state["tri"] % NT
                        state["tri"] += 1
                        if i not in tr_held:
                            break
                    else:
                        raise AssertionError("no free temp slot")
                    if hold:
                        tr_held.add(i)
                    return tr[i], ("tr", i)

                def TR(key):
                    tr_held.discard(key[1])

                for i in range(2):
                    op("dve", lambda h, i=i: h.memset(vE[i][:], 0.0), writes=[("vE", i)])
                    op("dve", lambda h, i=i: h.memset(vO[i][:], 0.0), writes=[("vO", i)])

                def setup_dmas():
                    if l > 0:
                        dma("sp", Wm[:, :, :], a_ws[l].rearrange("h t s -> t h s"), writes=["Wm"])
                        dma("sp", alng[:], a_ln_g[l:l + 1, :].partition_broadcast(128), writes=["alng"])
                        dma("sp", alnb[:], a_ln_b[l:l + 1, :].partition_broadcast(128), writes=["alnb"])
                    dma("sp", biasR[0:2, :].rearrange("p (b t) -> p b t", b=4), a_bias[l].rearrange("(b h) t -> h b t", h=2),
                        writes=["biasR"])

                def setup_crit():
                    op("pool", lambda h: h.memset(Ind[:], 1.0), writes=["Ind"])
                    op("pool", lambda h: h.affine_select(out=Ind[:], in_=Ind[:], pattern=[[1, 128]], compare_op=ALU.is_ge, fill=0.0,
                                                         base=0, channel_multiplier=-64), reads=["Ind"], writes=["Ind"])
                    op("pool", lambda h: h.affine_select(out=Ind[:], in_=Ind[:], pattern=[[-1, 128]], compare_op=ALU.is_ge, fill=0.0,
                                                         base=63, channel_multiplier=64), reads=["Ind"], writes=["Ind"])
                    op("pool", lambda h: h.affine_select(out=Wm[:, :, :], in_=Wm[:, :, :], pattern=[[0, 8], [-1, 128]],
                                                         compare_op=ALU.is_ge, fill=0.0, base=0, channel_multiplier=1),
                       reads=["Wm"], writes=["Wm"])
                    for hg in range(2):
                        b = fw.nextbank()

                        def f(h, b=b, hg=hg):
                            ins = None
                            for j in range(4):
                                ins = h.transpose(ps[b][:, j * 128:(j + 1) * 128], Wm[:, hg * 4 + j, :], ident[:])
                            return ins
                        op("pe", f, reads=["Wm", "ident"], writes=[("ps", b)])
                        op("act", lambda h, b=b, hg=hg: h.activation(
                            out=wsT[:, hg * 4:(hg + 1) * 4, :], in_=ps[b][:].rearrange("p (c t) -> p c t", c=4), func=AF.Copy),
                            writes=[("ps", b), "wsT"])

                def setup_dgs():
                    for c in range(2):
                        for j in range(NPE_CONV):
                            k = 31 - NPE_CONV + j
                            op("act", lambda h, c=c, j=j, k=k: h.activation(out=dgs[:, c * NPE_CONV + j, :], in_=identb[:], func=AF.Copy,
                                                                            scale=bcw(l, k, c)),
                               reads=["identb", "cA"], writes=["dgs"])

                def setup_late():
                    for q in range(NQ):
                        op("dve", lambda h, q=q: h.tensor_copy(
                            out=biasRs[0:2, :].rearrange("p (b q t) -> p b q t", b=4, q=NQ)[:, :, q, :],
                            in_=biasR[0:2, :].rearrange("p (b t) -> p b t", b=4)[:, :, 0:8]),
                           reads=["biasR"], writes=["biasRs"])
                    for q in range(NQ):
                        op("dve", lambda h, q=q: h.tensor_copy(out=Sel[0:8, q * 8:(q + 1) * 8], in_=ident[0:8, 0:8]),
                           reads=["ident"], writes=[("gv", 1)])
                    op("pool", lambda h: h.memset(Msk[:], 1.0), writes=[("gv", 0)])
                    op("pool", lambda h: h.affine_select(out=Msk[:].rearrange("p (q t) -> p q t", q=NQ),
                                                         in_=Msk[:].rearrange("p (q t) -> p q t", q=NQ),
                                                         pattern=[[-8, NQ], [0, 8]], compare_op=ALU.is_ge, fill=0.0,
                                                         base=0, channel_multiplier=1),
                       reads=[("gv", 0)], writes=[("gv", 0)])
                    op("pool", lambda h: h.affine_select(out=Msk[:].rearrange("p (q t) -> p q t", q=NQ),
                                                         in_=Msk[:].rearrange("p (q t) -> p q t", q=NQ),
                                                         pattern=[[8, NQ], [0, 8]], compare_op=ALU.is_ge, fill=0.0,
                                                         base=7, channel_multiplier=-1),
                       reads=[("gv", 0)], writes=[("gv", 0)])
                    for hg in range(2):
                        b = fw.nextbank()

                        def f(h, b=b, hg=hg):
                            ins = None
                            for j in range(4):
                                ins = h.matmul(ps[b][0:8, j * 128:(j + 1) * 128], lhsT=Wm[0:8, hg * 4 + j, 0:8], rhs=Sel[0:8, :],
                                               start=True, stop=True)
                            return ins
                        op("pe", f, reads=["Wm", ("gv", 1)], writes=[("ps", b)])
                        op("act", lambda h, b=b, hg=hg: h.activation(
                            out=sq32[0:8, hg * 512:(hg + 1) * 512], in_=ps[b][0:8, :],
                            func=AF.Copy), writes=[("ps", b), "sq"])
                    for hg in range(2):
                        b = fw.nextbank()

                        def f(h, b=b, hg=hg):
                            ins = None
                            for j in range(4):
                                ins = h.matmul(ps[b][:, j * 128:(j + 1) * 128], lhsT=Sel[0:8, :], rhs=sq32[0:8, hg * 512 + j * 128:hg * 512 + (j + 1) * 128],
                                               start=True, stop=True)
                            return ins
                        op("pe", f, reads=["sq", ("gv", 1)], writes=[("ps", b)])
                        for j in range(4):
                            op("dve", lambda h, b=b, hg=hg, j=j: h.tensor_tensor(
                                out=BD[:, hg * 4 + j, :], in0=ps[b][:, j * 128:(j + 1) * 128], in1=Msk[:], op=ALU.mult),
                                reads=[("gv", 0)], writes=[("ps", b), "BD"])

                def setup_sample_history():
                    for s4 in range(4):
                        dma("sp", hst[0:120, :], scb[l, s4 * 120:(s4 + 1) * 120, :], writes=["hst"])
                        b = fw.nextbank()

                        def f(h, b=b):
                            ins = None
                            for c in range(2):
                                ins = h.transpose(ps[b][:, c * 128:c * 128 + 120], hst[0:120, c * 128:(c + 1) * 128],
                                                  ident[0:120, 0:120])
                            return ins
                        op("pe", f, reads=["hst", "ident"], writes=[("ps", b)])
                        for c in range(2):
                            op("act", lambda h, b=b, c=c, s4=s4: h.activation(
                                out=GS[:, c, s4 * 4:(s4 + 1) * 4, 0:30],
                                in_=ps[b][:, c * 128:c * 128 + 120].rearrange("p (q t) -> p q t", q=4), func=AF.Copy),
                                writes=[("ps", b), "GS"])
                    dma("sp", hst[0:32, :], scc[l, :, :], writes=["hst"])
                    b = fw.nextbank()

                    def f(h, b=b):
                        ins = None
                        for c in range(2):
                            ins = h.transpose(ps[b][:, c * 128:c * 128 + 32], hst[0:32, c * 128:(c + 1) * 128], ident[0:32, 0:32])
                        return ins
                    op("pe", f, reads=["hst", "ident"], writes=[("ps", b)])
                    for c in range(2):
                        op("act", lambda h, b=b, c=c: h.activation(
                            out=CXS[:, c, :, 0:2], in_=ps[b][:, c * 128:c * 128 + 32].rearrange("p (q t) -> p q t", q=NQ),
                            func=AF.Copy), writes=[("ps", b), "CXS"])

                op("dve", lambda h: h.memset(G[0][:, :, 0:30], 0.0), writes=[("G", 0)])
                op("dve", lambda h: h.memset(GH[:, :, 0:30], 0.0), writes=["GH"])
                op("dve", lambda h: h.memset(GL[:, :, 0:30], 0.0), writes=["GL"])
                op("dve", lambda h: h.memset(CX[0][:, :, 0:2], 0.0), writes=[("CX", 0)])

                groups = [(g * NM, NM, "p") for g in range(TP // NM)] + [(TP, TS, "s")]
                NG = len(groups)
                GI = {}

                def ginfo(gi):
                    t0, n, kind = groups[gi]
                    par = gi % 2
                    samp = kind == "s"
                    d = dict(t0=t0, n=n, samp=samp, par=par, h=hb[par], hk=("hb", 0), cat=cat[par], ck=("cat", par))
                    if samp:
                        d["a3"] = lambda ap: ap.rearrange("p (q t) -> p q t", q=NQ)
                        d["gdst"] = lambda c: GS[:, c, :, 30:38]
                        d["gsrc"] = lambda c, k: GS[:, c, :, k:k + 8]
                        d["Gk"] = "GS"
                        d["cdst"] = lambda c: CXS[:, c, :, 2:10]
                        d["csrc"] = lambda c, k: CXS[:, c, :, k:k + 8]
                        d["Ck"] = "CXS"
                    else:
                        Gc, Cc = G[par], CX[par]
                        d["a3"] = lambda ap: ap
                        d["gdst"] = lambda c, Gc=Gc, n=n: Gc[:, c, 30:30 + n]
                        d["gsrc"] = lambda c, k, Gc=Gc, n=n: Gc[:, c, k:k + n]
                        d["Gk"] = ("G", par)
                        d["cdst"] = lambda c, Cc=Cc, n=n: Cc[:, c, 2:2 + n]
                        d["csrc"] = lambda c, k, Cc=Cc, n=n: Cc[:, c, k:k + n]
                        d["Ck"] = ("CX", par)
                        d["G"], d["CX"] = Gc, Cc
                    return d

                for gi in range(NG):
                    GI[gi] = ginfo(gi)

                def st_Npre(gi):
                    d = GI[gi]
                    n, t0 = d["n"], d["t0"]
                    xk = xkeys(t0, n)
                    tA, kA = T1(hold=True)
                    op("act", lambda h: h.activation(out=sq[:, :, :n], in_=xT[:, :, t0:t0 + n], func=AF.Square),
                       reads=xk, writes=["sq"])
                    b = fw.nextbank()
                    mmgroup(b, n, lambda k: ones_bf[:], lambda k: sq[:, k, :n], 8, ["sq", "ones_bf"])
                    op("act", lambda h: h.activation(out=tA[:, :n], in_=ps[b][:, :n], func=AF.Ln, scale=1.0 / D, bias=EPS),
                       writes=[("ps", b), kA])
                    op("act", lambda h: h.activation(out=tA[:, :n], in_=tA[:, :n], func=AF.Exp, scale=-0.5), reads=[kA], writes=[kA])
                    d["nrm"] = (tA, kA)

                def st_Npost(gi):
                    d = GI[gi]
                    n, t0, hcur = d["n"], d["t0"], d["h"]
                    xk = xkeys(t0, n)
                    tA, kA = d["nrm"]
                    for c in range(8):
                        op("dve", lambda h, c=c: h.scalar_tensor_tensor(out=hcur[:, c, :n], in0=xT[:, c, t0:t0 + n], scalar=g1(l, c),
                                                                       in1=tA[:, :n], op0=ALU.mult, op1=ALU.mult),
                           reads=xk + [kA, "cB"], writes=[d["hk"]])
                    TR(kA)

                def st_P(gi, wo_prev=None):
                    d = GI[gi]
                    n, hcur, hk = d["n"], d["h"], d["hk"]

                    def proj(bk, col0, c, **kw):
                        cb = 0 if col0 < 512 else (2 if col0 < 1536 else 3)
                        mmgroup(bk, n, lambda k: Win[:, k, col0 + c * 128:col0 + (c + 1) * 128], lambda k: hcur[:, k, :n], 8,
                                winkeys(cb) + [hk], off=c * n)
                    b1 = fw.nextbank()
                    for c in range(2):
                        proj(b1, 1792, c)
                    tcps = []
                    for c in range(2):
                        tcp, tck = T1(hold=True)
                        op("act", lambda h, c=c, tcp=tcp: h.activation(out=tcp[:, :n], in_=ps[b1][:, c * n:(c + 1) * n], func=AF.Copy),
                           writes=[("ps", b1), tck])
                        tcps.append((tcp, tck))
                    b2 = fw.nextbank(hold=True)
                    for c in range(2):
                        proj(b2, 2048, c)
                    b3 = fw.nextbank(hold=True)
                    for c in range(2):
                        proj(b3, 1536, c)
                    d["pc"] = (b2, b3, tcps)
                    ba = fw.nextbank(hold=True)
                    for c in range(2):
                        proj(ba, 1024, c)
                    bg = fw.nextbank()
                    for c in range(2):
                        proj(bg, 1280, c)
                    sgs = []
                    for c in range(2):
                        sg, sgk = T1(hold=True)
                        op("act", lambda h, c=c, sg=sg: h.activation(out=sg[:, :n], in_=ps[bg][:, c * n:(c + 1) * n], func=AF.Sigmoid),
                           writes=[("ps", bg), sgk])
                        sgs.append((sg, sgk))
                    d["pb"] = (ba, sgs)
                    d["vt"] = []
                    for tt in range(n // 128):
                        b = fw.nextbank()
                        mmgroup(b, 512, lambda k, tt=tt: hcur[:, k, tt * 128:(tt + 1) * 128], lambda k: Win[:, k, 512:1024], 8,
                                winkeys(1) + [hk])
                        vi = state["vsi"] % 2
                        state["vsi"] += 1
                        gv, gk = gvr[vi], ("gv", vi)
                        op("act", lambda h, b=b, gv=gv: h.activation(out=gv[:], in_=ps[b][:], func=AF.Gelu_apprx_tanh),
                           writes=[("ps", b), gk])
                        d["vt"].append((tt, vi))
                    for j in range(2):
                        b = fw.nextbank()
                        for c in range(2):
                            proj(b, (2 * j) * 128, c)
                        op("act", lambda h, b=b, j=j: h.activation(out=uT[:, 2 * j:2 * j + 2, :n],
                                                                   in_=ps[b][:, 0:2 * n].rearrange("p (c t) -> p c t", c=2),
                                                                   func=AF.Gelu_apprx_tanh),
                           writes=[("ps", b), "uT"])
                    if wo_prev is not None:
                        st_WOpe(wo_prev)

                def st_halo(gi):
                    d = GI[gi]
                    if d["samp"] or gi == 0:
                        return
                    Cc, Cp = d["CX"], GI[gi - 1]["CX"]
                    op("act", lambda h: h.activation(out=Cc[:, :, 0:2], in_=Cp[:, :, NM:NM + 2], func=AF.Copy),
                       reads=[GI[gi - 1]["Ck"]], writes=[d["Ck"]])
                    Gc, Gp = d["G"], GI[gi - 1]["G"]
                    op("act", lambda h: h.activation(out=Gc[:, :, 0:30], in_=Gp[:, :, NM:NM + 30], func=AF.Copy),
                       reads=[GI[gi - 1]["Gk"]], writes=[d["Gk"]])
                    if NPE_CONV > 0:
                        op("act", lambda h: h.activation(out=GH[:, :, 0:30], in_=GH[:, :, NM:NM + 30], func=AF.Copy),
                           reads=["GH"], writes=["GH"])
                        op("act", lambda h: h.activation(out=GL[:, :, 0:30], in_=GL[:, :, NM:NM + 30], func=AF.Copy),
                           reads=["GL"], writes=["GL"])

                def st_PCa(gi):
                    d = GI[gi]
                    n, a3, Ck = d["n"], d["a3"], d["Ck"]
                    cdst, csrc = d["cdst"], d["csrc"]
                    b2, b3, tcps = d["pc"]
                    for c in range(2):
                        tcp, tck = tcps[c]
                        op("dve", lambda h, c=c, tcp=tcp: h.tensor_tensor(out=cdst(c), in0=a3(ps[b2][:, c * n:(c + 1) * n]),
                                                                         in1=a3(tcp[:, :n]), op=ALU.mult),
                           reads=[tck], writes=[("ps", b2), Ck])
                        TR(tck)
                    fw.release(b2)
                    for c in range(2):
                        if ENG_CONVC == "dve":
                            op("dve", lambda h, c=c: h.tensor_scalar(out=a3(accC[:, c, :n]), in0=csrc(c, 0), scalar1=ccw(l, 0, c),
                                                                     scalar2=None, op0=ALU.mult),
                               reads=[Ck, "cB"], writes=[("accC", c)])
                            for k in range(1, 3):
                                op("dve", lambda h, c=c, k=k: h.scalar_tensor_tensor(out=a3(accC[:, c, :n]), in0=csrc(c, k),
                                                                                    scalar=ccw(l, k, c), in1=a3(accC[:, c, :n]),
                                                                                    op0=ALU.mult, op1=ALU.add),
                                   reads=[Ck, "cB", ("accC", c)], writes=[("accC", c)])
                            continue
                        op("pool", lambda h, c=c: h.tensor_scalar(out=a3(accC[:, c, :n]), in0=csrc(c, 0), scalar1=ccw(l, 0, c),
                                                                  scalar2=0.0, op0=ALU.mult, op1=ALU.add),
                           reads=[Ck, "cB"], writes=[("accC", c)])
                        for k in range(1, 3):
                            ctp, ctk = T1()
                            op("pool", lambda h, c=c, k=k, ctp=ctp: h.tensor_scalar(out=a3(ctp[:, :n]), in0=csrc(c, k),
                                                                                   scalar1=ccw(l, k, c), scalar2=0.0,
                                                                                   op0=ALU.mult, op1=ALU.add),
                               reads=[Ck, "cB"], writes=[ctk])
                            op("pool", lambda h, c=c, ctp=ctp: h.tensor_tensor(out=accC[:, c, :n], in0=accC[:, c, :n], in1=ctp[:, :n],
                                                                              op=ALU.add),
                               reads=[ctk, ("accC", c)], writes=[("accC", c)])

                def st_PCb(gi):
                    d = GI[gi]
                    n, ccur, ck = d["n"], d["cat"], d["ck"]
                    b2, b3, tcps = d["pc"]
                    op("dve", lambda h: h.tensor_tensor(out=ccur[:, 6:8, :n], in0=ps[b3][:, 0:2 * n].rearrange("p (c t) -> p c t", c=2),
                                                        in1=accC[:, :, :n], op=ALU.mult),
                       reads=[("accC", 0), ("accC", 1)], writes=[("ps", b3), ck])
                    fw.release(b3)

                def st_PBdve(gi):
                    d = GI[gi]
                    n, a3, Gk, gdst = d["n"], d["a3"], d["Gk"], d["gdst"]
                    ba, sgs = d["pb"]
                    for c in range(2):
                        sg, sgk = sgs[c]
                        op("dve", lambda h, sg=sg, c=c: h.tensor_tensor(out=gdst(c), in0=a3(ps[ba][:, c * n:(c + 1) * n]),
                                                                       in1=a3(sg[:, :n]), op=ALU.mult),
                           reads=[sgk], writes=[("ps", ba), Gk])
                        TR(sgk)
                    fw.release(ba)
                    if NPE_CONV > 0 and not d["samp"]:
                        Gc = d["G"]
                        op("dve", lambda h: h.tensor_copy(out=GH[:, :, 30:30 + n], in_=Gc[:, :, 30:30 + n]),
                           reads=[Gk], writes=["GH"])
                        op("dve", lambda h: h.tensor_tensor(out=GL[:, :, 30:30 + n], in0=Gc[:, :, 30:30 + n], in1=GH[:, :, 30:30 + n],
                                                            op=ALU.subtract),
                           reads=[Gk, "GH"], writes=["GL"])

                def v4(ap):
                    return ap.rearrange("p (b h d) -> p b h d", b=4, h=2)

                def st_LNa(gi):
                    d = GI[gi]
                    d["vs"] = []
                    for (tt, vi) in d["vt"]:
                        gv, gk = gvr[vi], ("gv", vi)
                        state["vst"] = state.get("vst", 0) + 1
                        st_, sk = vst[state["vst"] % 4], ("vst", state["vst"] % 4)
                        d["vs"].append((st_, sk))
                        op("dve", lambda h, gv=gv, st_=st_: h.bn_stats(out=st_[:, 0:6], in_=gv[:]), reads=[gk], writes=[sk])
                        op("dve", lambda h, st_=st_: h.bn_aggr(out=st_[:, 6:8], in_=st_[:, 0:6]), reads=[sk], writes=[sk])
                        op("dve", lambda h, st_=st_: h.tensor_scalar(out=st_[:, 8:9], in0=st_[:, 7:8], scalar1=EPS, scalar2=None,
                                                                    op0=ALU.add), reads=[sk], writes=[sk])
                        op("pool", lambda h, st_=st_: h.tensor_tensor(out=st_[:, 9:10], in0=st_[:, 8:9], in1=mhalf[:, 0:1],
                                                                     op=ALU.pow), reads=[sk, "mhalf"], writes=[sk])

                def st_LNb(gi):
                    d = GI[gi]
                    samp = d["samp"]
                    for j, (tt, vi) in enumerate(d["vt"]):
                        gv, gk = gvr[vi], ("gv", vi)
                        st_, sk = d["vs"][j]
                        op("dve", lambda h, gv=gv, st_=st_: h.tensor_scalar(out=gv[:], in0=gv[:], scalar1=st_[:, 6:7],
                                                                           scalar2=st_[:, 9:10], op0=ALU.subtract, op1=ALU.mult),
                           reads=[gk, sk], writes=[gk])
                        op(ENG_LNAFF, lambda h, gv=gv: h.tensor_tensor(out=gv[:], in0=gv[:], in1=alng[:], op=ALU.mult),
                           reads=[gk, "alng"], writes=[gk])
                        vEc, vOc = vE[vi], vO[vi]
                        vEk, vOk = ("vE", vi), ("vO", vi)
                        if not samp:
                            op(ENG_LNAFF, lambda h, gv=gv, vEc=vEc: h.tensor_tensor(out=v4(vEc[:])[:, :, 0, :], in0=v4(gv[:])[:, :, 0, :],
                                                                                in1=v4(alnb[:])[:, :, 0, :], op=ALU.add),
                               reads=[gk, "alnb"], writes=[vEk])
                            op(ENG_LNAFF, lambda h, gv=gv, vOc=vOc: h.tensor_tensor(out=v4(vOc[:])[:, :, 1, :], in0=v4(gv[:])[:, :, 1, :],
                                                                                in1=v4(alnb[:])[:, :, 1, :], op=ALU.add),
                               reads=[gk, "alnb"], writes=[vOk])
                        else:
                            op(ENG_LNAFF, lambda h, gv=gv: h.tensor_tensor(out=vf[:], in0=gv[:], in1=alnb[:], op=ALU.add),
                               reads=[gk, "alnb"], writes=["vf"])
                            dma("sp", vs_o[l, :, :], vf[:], reads=["vf"], writes=[("o_vs", l)])
                            op("act", lambda h, vEc=vEc: h.activation(out=v4(vEc[:])[:, :, 0, :], in_=v4(vf[:])[:, :, 0, :],
                                                                     func=AF.Copy), reads=["vf"], writes=[vEk])
                            op("act", lambda h, vOc=vOc: h.activation(out=v4(vOc[:])[:, :, 1, :], in_=v4(vf[:])[:, :, 1, :],
                                                                     func=AF.Copy), reads=["vf"], writes=[vOk])

                def st_spatial(gi):
                    d = GI[gi]
                    samp, ccur, ck = d["samp"], d["cat"], d["ck"]
                    d["sp"] = []
                    for (tt, vi) in d["vt"]:
                        vEc, vOc = vE[vi], vO[vi]
                        vEk, vOk = ("vE", vi), ("vO", vi)
                        b = fw.nextbank(hold=True)
                        Wmix = BD if samp else wsT

                        def f(h, b=b, vEc=vEc, vOc=vOc, Wmix=Wmix):
                            ins = None
                            h.matmul(ps[b][:, :], lhsT=Ind[0:2, :], rhs=(biasRs if samp else biasR)[0:2, :], start=True, stop=False)
                            for blk in range(4):
                                h.matmul(ps[b][:, blk * 128:(blk + 1) * 128], lhsT=vEc[:, blk * 128:(blk + 1) * 128],
                                         rhs=Wmix[:, 2 * blk, :], start=False, stop=False)
                                ins = h.matmul(ps[b][:, blk * 128:(blk + 1) * 128], lhsT=vOc[:, blk * 128:(blk + 1) * 128],
                                               rhs=Wmix[:, 2 * blk + 1, :], start=False, stop=(blk == 3))
                            return ins
                        op("pe", f, reads=[vEk, vOk, "BD" if samp else "wsT", "Ind", "biasR", "biasRs"], writes=[("ps", b)])
                        d["sp"].append((tt, vi, b))

                def st_ya(gi):
                    d = GI[gi]
                    samp, ccur, ck = d["samp"], d["cat"], d["ck"]
                    for (tt, vi, b) in d["sp"]:
                        op("dve", lambda h, b=b, tt=tt: h.tensor_tensor(
                            out=ccur[:, 0:4, tt * 128:(tt + 1) * 128], in0=ps[b][:].rearrange("p (b t) -> p b t", b=4),
                            in1=uT[:, 0:4, tt * 128:(tt + 1) * 128], op=ALU.mult),
                            reads=["uT"], writes=[("ps", b), ck])
                        fw.release(b)

                def st_CBpool(gi):
                    d = GI[gi]
                    n, a3, Gk, gsrc = d["n"], d["a3"], d["Gk"], d["gsrc"]
                    KP = 31 - NPOOL
                    if NPOOL == 0:
                        return
                    ptp, ptk = T1()
                    for c in range(2):
                        op("pool", lambda h, c=c: h.tensor_scalar(out=a3(sqb[:, c, :n]), in0=gsrc(c, KP), scalar1=bcw(l, KP, c),
                                                                  scalar2=0.0, op0=ALU.mult, op1=ALU.add),
                           reads=[Gk, "cA"], writes=[("sqb", c)])
                        for k in range(KP + 1, 31):
                            op("pool", lambda h, c=c, k=k: h.tensor_scalar(out=a3(ptp[:, :n]), in0=gsrc(c, k), scalar1=bcw(l, k, c),
                                                                           scalar2=0.0, op0=ALU.mult, op1=ALU.add),
                               reads=[Gk, "cA"], writes=[ptk])
                            op("pool", lambda h, c=c: h.tensor_tensor(out=sqb[:, c, :n], in0=sqb[:, c, :n], in1=ptp[:, :n], op=ALU.add),
                               reads=[ptk, ("sqb", c)], writes=[("sqb", c)])

                def conv_split(gi):
                    d = GI[gi]
                    KD = 31 - NPOOL
                    if NPE_CONV > 0 and not d["samp"]:
                        KD = 31 - NPE_CONV
                    return KD, max(1, (KD * 2) // 3)

                def st_CBpe(gi):
                    d = GI[gi]
                    if NPE_CONV == 0 or d["samp"]:
                        return
                    n = d["n"]
                    bcv = fw.nextbank(hold=True)
                    d["cvb"] = bcv
                    taps = list(range(31 - NPE_CONV, 31))
                    for c in range(2):
                        def f(h, c=c):
                            ins = None
                            for j, k in enumerate(taps):
                                dg = dgs[:, c * NPE_CONV + j, :]
                                h.matmul(ps[bcv][:, c * n:(c + 1) * n], lhsT=dg, rhs=GH[:, c, k:k + n], start=(j == 0), stop=False)
                                ins = h.matmul(ps[bcv][:, c * n:(c + 1) * n], lhsT=dg, rhs=GL[:, c, k:k + n], start=False,
                                               stop=(j == len(taps) - 1))
                            return ins
                        op("pe", f, reads=["dgs", "GH", "GL"], writes=[("ps", bcv)])

                def st_CBinit(gi):
                    d = GI[gi]
                    n, a3, Gk, gsrc = d["n"], d["a3"], d["Gk"], d["gsrc"]
                    for c in range(2):
                        op("dve", lambda h, c=c: h.tensor_scalar(out=a3(acc[:, c, :n]), in0=gsrc(c, 0), scalar1=bcw(l, 0, c),
                                                                 scalar2=bcb(l, c), op0=ALU.mult, op1=ALU.add),
                           reads=[Gk, "cA"], writes=[("acc", c)])
                    if "cvb" in d:
                        bcv = d["cvb"]
                        op("dve", lambda h: h.tensor_tensor(out=acc[:, :, :n], in0=ps[bcv][:, 0:2 * n].rearrange("p (c t) -> p c t", c=2),
                                                            in1=acc[:, :, :n], op=ALU.add),
                           reads=[("acc", 0), ("acc", 1)], writes=[("ps", bcv), ("acc", 0), ("acc", 1)])
                        fw.release(bcv)

                def st_CBdve(gi, part, merge):
                    d = GI[gi]
                    n, a3, Gk, gsrc = d["n"], d["a3"], d["Gk"], d["gsrc"]
                    KD, K1 = conv_split(gi)
                    ks = range(1, K1) if part == 0 else range(K1, KD)
                    for k in ks:
                        for c in range(2):
                            op("dve", lambda h, k=k, c=c: h.scalar_tensor_tensor(out=a3(acc[:, c, :n]), in0=gsrc(c, k),
                                                                                scalar=bcw(l, k, c), in1=a3(acc[:, c, :n]),
                                                                                op0=ALU.mult, op1=ALU.add),
                               reads=[Gk, "cA", ("acc", c)], writes=[("acc", c)])

                ACCK = [("acc", 0), ("acc", 1)]

                def st_LB1(gi):
                    d = GI[gi]
                    n = d["n"]
                    op("act", lambda h: h.activation(out=sqb[:, :, :n], in_=acc[:, :, :n], func=AF.Square),
                       reads=ACCK, writes=[("sqb", 0), ("sqb", 1)])
                    b1 = fw.nextbank()
                    mmgroup(b1, n, lambda k: ones32[:], lambda k: acc[:, k, :n], 2, ACCK + ["ones32"])
                    b2 = fw.nextbank()
                    mmgroup(b2, n, lambda k: ones32[:], lambda k: sqb[:, k, :n], 2, [("sqb", 0), ("sqb", 1), "ones32"])
                    mean, mk_ = T1(hold=True)
                    ex2, ek_ = T1(hold=True)
                    op("act", lambda h: h.activation(out=mean[:, :n], in_=ps[b1][:, :n], func=AF.Identity, scale=1.0 / DB),
                       writes=[("ps", b1), mk_])
                    op("act", lambda h: h.activation(out=ex2[:, :n], in_=ps[b2][:, :n], func=AF.Identity, scale=1.0 / DB, bias=EPS),
                       writes=[("ps", b2), ek_])
                    d["lb"] = (mean, mk_, ex2, ek_)

                def st_LB2a(gi):
                    d = GI[gi]
                    n = d["n"]
                    mean, mk_, ex2, ek_ = d["lb"]
                    m2, m2k = T1(hold=True)
                    op("dve", lambda h: h.tensor_tensor(out=m2[:, :n], in0=mean[:, :n], in1=mean[:, :n], op=ALU.mult),
                       reads=[mk_], writes=[m2k])
                    op("dve", lambda h: h.tensor_tensor(out=m2[:, :n], in0=ex2[:, :n], in1=m2[:, :n], op=ALU.subtract),
                       reads=[ek_, m2k], writes=[m2k])
                    op("act", lambda h: h.activation(out=m2[:, :n], in_=m2[:, :n], func=AF.Ln), reads=[m2k], writes=[m2k])
                    op("act", lambda h: h.activation(out=m2[:, :n], in_=m2[:, :n], func=AF.Exp, scale=-0.5), reads=[m2k], writes=[m2k])
                    d["lb2"] = (m2, m2k)

                def st_LB2b(gi):
                    d = GI[gi]
                    n, ccur, ck = d["n"], d["cat"], d["ck"]
                    mean, mk_, ex2, ek_ = d["lb"]
                    m2, m2k = d["lb2"]
                    rsb, rk_ = m2, m2k
                    for c in range(2):
                        op("dve", lambda h, c=c: h.tensor_tensor(out=acc[:, c, :n], in0=acc[:, c, :n], in1=mean[:, :n],
                                                                 op=ALU.subtract), reads=[("acc", c), mk_], writes=[("acc", c)])
                        op("dve", lambda h, c=c: h.tensor_tensor(out=acc[:, c, :n], in0=acc[:, c, :n], in1=rsb[:, :n],
                                                                 op=ALU.mult), reads=[("acc", c), rk_], writes=[("acc", c)])
                        op("act", lambda h, c=c: h.activation(out=ccur[:, 4 + c, :n], in_=acc[:, c, :n], func=AF.Silu,
                                                              scale=blg(l, c), bias=blb(l, c)),
                           reads=[("acc", c), "cB"], writes=[ck])
                    TR(mk_)
                    TR(ek_)
                    TR(m2k)

                def st_WOpe(gi):
                    d = GI[gi]
                    n, ccur, ck = d["n"], d["cat"], d["ck"]
                    d["wo"] = {}
                    for j in range(4):
                        b = fw.nextbank(hold=True)
                        for c in range(2):
                            m = 2 * j + c
                            mmgroup(b, n, lambda k, m=m: Wout[:, k, m * 128:(m + 1) * 128], lambda k: ccur[:, k, :n], 8,
                                    WOUTK + [ck], off=c * n)
                        d["wo"][j] = b

                def st_WOdve(gi):
                    d = GI[gi]
                    n, t0 = d["n"], d["t0"]
                    xk = xkeys(t0, n)
                    for j in range(4):
                        b = d["wo"][j]
                        op("dve", lambda h, b=b, j=j: h.tensor_tensor(out=xT[:, 2 * j:2 * j + 2, t0:t0 + n],
                                                                     in0=ps[b][:, 0:2 * n].rearrange("p (c t) -> p c t", c=2),
                                                                     in1=xT[:, 2 * j:2 * j + 2, t0:t0 + n], op=ALU.add),
                           reads=xk, writes=[("ps", b)] + xk)
                        fw.release(b)

                def st_state(gi):
                    d = GI[gi]
                    if gi == TP // NM - 1:
                        Gc, Cc, Gk, Ck = d["G"], d["CX"], d["Gk"], d["Ck"]
                        so, sok = T1()
                        b = fw.nextbank()

                        def f(h):
                            ins = None
                            for c in range(2):
                                ins = h.transpose(ps[b][0:30, c * 128:(c + 1) * 128], Gc[:, c, NM:NM + 30], ident[:])
                            return ins
                        op("pe", f, reads=[Gk, "ident"], writes=[("ps", b)])
                        op("act", lambda h: h.activation(out=so[0:30, :], in_=ps[b][0:30, 0:256], func=AF.Copy),
                           writes=[("ps", b), sok])
                        dma("sp", ncbp_o[l, :, :], so[0:30, :], reads=[sok], writes=[("o_ncbp", l)])
                        so2, sok2 = T1()
                        b2 = fw.nextbank()

                        def f(h):
                            ins = None
                            for c in range(2):
                                ins = h.transpose(ps[b2][0:2, c * 128:(c + 1) * 128], Cc[:, c, NM:NM + 2], ident[:])
                            return ins
                        op("pe", f, reads=[Ck, "ident"], writes=[("ps", b2)])
                        op("act", lambda h: h.activation(out=so2[0:2, :], in_=ps[b2][0:2, 0:256], func=AF.Copy),
                           writes=[("ps", b2), sok2])
                        dma("sp", nccp_o[l, :, :], so2[0:2, :], reads=[sok2], writes=[("o_nccp", l)])
                    if d["samp"]:
                        for c in range(2):
                            op("act", lambda h, c=c: h.activation(out=sq32[:, c * 480:(c + 1) * 480].rearrange("p (q t) -> p q t", q=NQ),
                                                                  in_=GS[:, c, :, 8:38], func=AF.Copy),
                               reads=["GS"], writes=["sq"])
                        op("act", lambda h: h.activation(out=gvr[0][:, 0:64].rearrange("p (c q t) -> p c q t", c=2, q=NQ),
                                                         in_=CXS[:, :, :, 8:10], func=AF.Copy),
                           reads=["CXS"], writes=[("gv", 0)])
                        for s4 in range(4):
                            b = fw.nextbank()
                            so, sok = T1()

                            def f(h, b=b, s4=s4):
                                ins = None
                                for c in range(2):
                                    ins = h.transpose(ps[b][0:120, c * 128:(c + 1) * 128], sq32[:, c * 480 + s4 * 120:c * 480 + (s4 + 1) * 120], ident[:])
                                return ins
                            op("pe", f, reads=["sq", "ident"], writes=[("ps", b)])
                            op("act", lambda h, b=b, so=so: h.activation(out=so[0:120, :], in_=ps[b][0:120, 0:256], func=AF.Copy),
                               writes=[("ps", b), sok])
                            dma("sp", ncbs_o[l, s4 * 120:(s4 + 1) * 120, :], so[0:120, :], reads=[sok],
                                writes=[("o_ncbs", l, s4)])
                        b = fw.nextbank()
                        so, sok = T1()

                        def f(h, b=b):
                            ins = None
                            for c in range(2):
                                ins = h.transpose(ps[b][0:32, c * 128:(c + 1) * 128], gvr[0][:, c * 32:(c + 1) * 32], ident[:])
                            return ins
                        op("pe", f, reads=[("gv", 0), "ident"], writes=[("ps", b)])
                        op("act", lambda h, b=b, so=so: h.activation(out=so[0:32, :], in_=ps[b][0:32, 0:256], func=AF.Copy),
                           writes=[("ps", b), sok])
                        dma("sp", nccs_o[l, :, :], so[0:32, :], reads=[sok], writes=[("o_nccs", l)])

                setup_dmas()
                load_mixer_weights(l, part="wout")
                st_Npre(0)
                st_Npost(0)
                st_P(0)
                st_PCa(0)
                st_PBdve(0)
                st_LNa(0)
                st_LNb(0)
                setup_crit()
                st_spatial(0)
                st_PCb(0)
                st_ya(0)
                setup_dgs()
                st_Npre(1)
                st_Npost(1)
                st_CBpe(0)
                setup_late()
                for gi in range(NG):
                    nx = gi + 1 if gi + 1 < NG else None
                    n2 = gi + 2 if gi + 2 < NG else None
                    if gi == 2:
                        setup_sample_history()
                    st_CBinit(gi)
                    if nx is not None:
                        st_halo(nx)
                        st_P(nx, wo_prev=(gi - 1 if gi > 0 else None))
                    elif gi > 0:
                        st_WOpe(gi - 1)
                    if n2 is not None:
                        st_Npre(n2)
                    st_CBdve(gi, 0, False)
                    if nx is not None:
                        st_PCa(nx)
                        st_PBdve(nx)
                        st_LNa(nx)
                    st_CBdve(gi, 1, True)
                    if gi > 0:
                        st_WOdve(gi - 1)
                        st_state(gi - 1)
                    st_LB1(gi)
                    if nx is not None:
                        st_LNb(nx)
                        st_CBpe(nx)
                        st_spatial(nx)
                        st_PCb(nx)
                    if n2 is not None:
                        st_Npost(n2)
                    st_LB2a(gi)
                    if nx is not None:
                        st_ya(nx)
                    st_LB2b(gi)
                load_w1(0, extra=[("win", kk, cb) for kk in range(8) for cb in range(4)])
                ff_w1_0_prefetched = True
                st_WOpe(NG - 1)
                st_WOdve(NG - 1)
                st_state(NG - 1)
                fw.barrier()

            with ExitStack() as sf:
                h2T = sb(sf, "h2T", [128, 8, T], BF16)
                ablk = sb(sf, "ablk", [128, 8, T], BF16)
                W2 = RW[:, 16384:24576].rearrange("p (k n) -> p k n", k=8)
                o = 24576
                sq2 = RW[:, o:o + 4096].rearrange("p (k n) -> p k n", k=8)
                tA2 = RW[:, o + 4096:o + 5120].bitcast(F32)
                tB2 = RW[:, o + 5120:o + 6144].bitcast(F32)
                rl = [RW[:, o + 6144 + i * 512:o + 6144 + (i + 1) * 512] for i in range(4)]
                fgroups = [(0, 436), (436, 435), (871, 435), (1306, 435), (1741, 435)]

                def load_w2(blk):
                    for k in range(8):
                        dma("pool", W2[:, k, :], w_ff2[l, blk * 1024 + k * 128:blk * 1024 + (k + 1) * 128, :],
                            writes=[("w2", k)])

                load_w2(0)
                load_w1(1)
                def ffnorm(gi):
                    t0, n = fgroups[gi]
                    norm(t0, n, lambda c: g2(l, c), lambda c, t0=t0, n=n: h2T[:, c, t0:t0 + n], ("h2", gi), sq2, tA2, tB2, "sq2", "tA2", "tB2")
                ffnorm(0)
                ffnorm(1)
                rli = 0
                for blk in range(4):
                    W = W1[blk % 2]
                    W1K = [("w1", blk % 2, k) for k in range(8)]
                    W2K = [("w2", k) for k in range(8)]
                    for gi, (t0, n) in enumerate(fgroups):
                        for hc in range(8):
                            b = fw.nextbank()
                            mmgroup(b, n, lambda k, hc=hc, W=W: W[:, k, hc * 128:(hc + 1) * 128],
                                    lambda k, t0=t0, n=n: h2T[:, k, t0:t0 + n], 8, W1K + [("h2", gi)])
                            r = rl[rli % 4]
                            rk = ("rl", rli % 4)
                            rli += 1
                            op("act", lambda h, b=b, r=r, n=n: h.activation(out=r[:, :n], in_=ps[b][:, :n], func=AF.Relu),
                               writes=[("ps", b), rk])
                            op("pool", lambda h, r=r, hc=hc, t0=t0, n=n: h.tensor_tensor(out=ablk[:, hc, t0:t0 + n], in0=r[:, :n],
                                                                                       in1=r[:, :n], op=ALU.mult),
                               reads=[rk], writes=[("a", gi)])
                        if blk == 0 and gi + 2 < len(fgroups):
                            ffnorm(gi + 2)
                    if blk == 3 and l + 1 < L:
                        allw1 = [("w1", pp, kk) for pp in range(2) for kk in range(8)]
                        for k in range(7):
                            dma("pool", Win[:, k, :], w_in[l + 1, k * 128:(k + 1) * 128, :], writes=[("win", k, cb) for cb in range(4)] + allw1)
                    for gi, (t0, n) in enumerate(fgroups):
                        xk = xkeys(t0, n)
                        for m in range(8):
                            b = fw.nextbank()
                            mmgroup(b, n, lambda k, m=m: W2[:, k, m * 128:(m + 1) * 128],
                                    lambda k, t0=t0, n=n: ablk[:, k, t0:t0 + n], 8, W2K + [("a", gi)])
                            op("dve", lambda h, b=b, m=m, t0=t0, n=n: h.tensor_tensor(out=xT[:, m, t0:t0 + n], in0=ps[b][:, :n],
                                                                                     in1=xT[:, m, t0:t0 + n], op=ALU.add),
                               reads=xk, writes=[("ps", b)] + xk)
                    if blk + 1 < 4:
                        load_w2(blk + 1)
                    if blk + 2 < 4:
                        load_w1(blk + 2)
                fw.barrier(keep_pool_dma=True)

        with ExitStack() as so:
            sq3 = sb(so, "sq3", [128, 8, 512], BF16)
            tA3 = sb(so, "tA3", [128, 512], F32)
            tB3 = sb(so, "tB3", [128, 512], F32)
            yT = [sb(so, "yT%d" % i, [128, 8, 512], F32) for i in range(2)]
            ost = [sb(so, "ost%d" % i, [128, D], F32) for i in range(2)]
            fgroups = [(g * 512, 512) for g in range(4)] + [(TP, TS)]
            ost = ost + [sb(so, "ostx%d" % i, [128, D], F32) for i in range(2)]
            NOST = len(ost)
            oi = [0]

            def fin_pre(gi):
                t0, n = fgroups[gi]
                xk = xkeys(t0, n)
                op("act", lambda h: h.activation(out=sq3[:, :, :n], in_=xT[:, :, t0:t0 + n], func=AF.Square),
                   reads=xk, writes=["sq3"])
                b = fw.nextbank()
                mmgroup(b, n, lambda k: ones_bf[:], lambda k: sq3[:, k, :n], 8, ["sq3", "ones_bf"])
                op("act", lambda h: h.activation(out=tA3[:, :n], in_=ps[b][:, :n], func=AF.Ln, scale=1.0 / D, bias=EPS),
                   writes=[("ps", b), "tA3"])

            def fin_post(gi):
                t0, n = fgroups[gi]
                xk = xkeys(t0, n)
                yc = yT[gi % 2]
                op("act", lambda h: h.activation(out=tB3[:, :n], in_=tA3[:, :n], func=AF.Exp, scale=-0.5), reads=["tA3"], writes=["tB3"])
                for c in range(8):
                    op("dve", lambda h, c=c: h.scalar_tensor_tensor(out=yc[:, c, :n], in0=xT[:, c, t0:t0 + n], scalar=gf(0, c),
                                                                   in1=tB3[:, :n], op0=ALU.mult, op1=ALU.mult),
                       reads=xk + ["tB3", "cB"], writes=[("yT", gi % 2)])

            def fin_out(gi):
                t0, n = fgroups[gi]
                yc = yT[gi % 2]
                for tt in range(n // 128):
                    os_ = ost[oi[0] % NOST]
                    ok = ("ost", oi[0] % NOST)
                    oi[0] += 1
                    for cg in range(2):
                        b = fw.nextbank()

                        def f(h, b=b, cg=cg, tt=tt):
                            ins = None
                            for j in range(4):
                                ins = h.transpose(ps[b][:, j * 128:(j + 1) * 128], yc[:, cg * 4 + j, tt * 128:(tt + 1) * 128], ident[:])
                            return ins
                        op("pe", f, reads=[("yT", gi % 2), "ident"], writes=[("ps", b)])
                        if cg == 0:
                            op("act", lambda h, b=b, os_=os_: h.activation(out=os_[:, 0:512], in_=ps[b][:], func=AF.Copy),
                               writes=[("ps", b), ok])
                        else:
                            op("dve", lambda h, b=b, os_=os_: h.tensor_copy(out=os_[:, 512:1024], in_=ps[b][:]),
                               writes=[("ps", b), ok])
                    r0 = t0 + tt * 128
                    dma("sp", y_o[r0:r0 + 128, :], os_[:], reads=[ok], writes=[("o_y", r0)])

            fin_pre(0)
            fin_post(0)
            for gi in range(len(fgroups)):
                if gi + 1 < len(fgroups):
                    fin_pre(gi + 1)
                    fin_post(gi + 1)
                fin_out(gi)
            fw.barrier(only=["sp"])
        fw.emit()
    return nc


_NC_CACHE = {}


def kernel(x_prompt, x_sample, state_conv_b, state_conv_c, norm1_g, w_in, a_ln_g, a_ln_b, a_ws, a_bias,
           b_conv_w, b_conv_b, b_ln_g, b_ln_b, c_conv_w, w_out, norm2_g, w_ff1, w_ff2, norm_f_g):
    f32 = lambda a: np.ascontiguousarray(np.asarray(a, dtype=np.float32))
    x_prompt, x_sample = f32(x_prompt), f32(x_sample)
    state_conv_b, state_conv_c = f32(state_conv_b), f32(state_conv_c)
    shared = dict(norm1_g=f32(norm1_g), w_in=f32(w_in), a_ln_g=f32(a_ln_g), a_ln_b=f32(a_ln_b), a_ws=f32(a_ws),
                  a_bias=f32(a_bias), b_conv_w=f32(b_conv_w), b_conv_b=f32(b_conv_b), b_ln_g=f32(b_ln_g),
                  b_ln_b=f32(b_ln_b), c_conv_w=f32(c_conv_w), w_out=f32(w_out), norm2_g=f32(norm2_g),
                  w_ff1=f32(w_ff1), w_ff2=f32(w_ff2), norm_f_g=f32(norm_f_g))
    in_maps = []
    for c in range(NCORES):
        xs = x_sample[c * NQ:(c + 1) * NQ].reshape(TS, D)
        m = dict(shared)
        m["xin"] = np.ascontiguousarray(np.concatenate([x_prompt[c], xs], axis=0))
        m["scb"] = np.ascontiguousarray(state_conv_b[:, c * NQ:(c + 1) * NQ].reshape(L, NQ * 30, DB))
        m["scc"] = np.ascontiguousarray(state_conv_c[:, c * NQ:(c + 1) * NQ].reshape(L, NQ * 2, DC))
        in_maps.append(m)
    if "nc" not in _NC_CACHE:
        _NC_CACHE["nc"] = build_nc()
    nc = _NC_CACHE["nc"]
    res = run_bass_kernel_spmd(nc, in_maps, core_ids=list(range(NCORES)))
    R = res.results
    y_prompt = np.stack([R[c]["y"][:TP] for c in range(NCORES)], axis=0)
    y_sample = np.concatenate([R[c]["y"][TP:].reshape(NQ, 8, D) for c in range(NCORES)], axis=0)
    ncb_p = np.stack([R[c]["ncb_p"] for c in range(NCORES)], axis=1)
    ncc_p = np.stack([R[c]["ncc_p"] for c in range(NCORES)], axis=1)
    ncb_s = np.concatenate([R[c]["ncb_s"].reshape(L, NQ, 30, DB) for c in range(NCORES)], axis=1)
    ncc_s = np.concatenate([R[c]["ncc_s"].reshape(L, NQ, 2, DC) for c in range(NCORES)], axis=1)
    v_s = np.concatenate([R[c]["v_s"].reshape(L, NQ, 8, DA) for c in range(NCORES)], axis=1)
    return (y_prompt.astype(np.float32), y_sample.astype(np.float32), ncb_p.astype(np.float32), ncc_p.astype(np.float32),
            ncb_s.astype(np.float32), ncc_s.astype(np.float32), v_s.astype(np.float32))
```

```python
import numpy as np
from contextlib import ExitStack
import concourse.bass as bass
import concourse.mybir as mybir
from concourse.bass_utils import run_bass_kernel_spmd

F32 = mybir.dt.float32
BF16 = mybir.dt.bfloat16
AF = mybir.ActivationFunctionType
ALU = mybir.AluOpType

D = 1024
DA = 512
DB = 256
DC = 256
DIN = 2304
DFF = 4096
L = 2
TP = 2048
TS = 128
T = TP + TS
NQ = 16
EPS = 1e-6
NCORES = 8
RELAX_SAME_ENGINE_RAW = False
NPE_CONV = 16
ENG_LNAFF = "pool"
ENG_CONVC = "pool"


class Rec:
    def __init__(self):
        self.calls = []

    def __getattr__(self, name):
        def m(*a, **k):
            self.calls.append((name, a, k))
            return self
        return m


class FW:
    def __init__(self, nc, es, ndma=28):
        self.nc = nc
        hs = dict(pe=nc.tensor, act=nc.scalar, dve=nc.vector, pool=nc.gpsimd, sp=nc.sync)
        self.E = {}
        for k, h in hs.items():
            self.E[k] = dict(h=h, sem=es.enter_context(nc.semaphore("s_" + k)), cnt=0, seen={}, prog=[])
        self.dsem = [es.enter_context(nc.semaphore("dq%d" % i)) for i in range(ndma)]
        self.dval = [0] * ndma
        self.dpool = {"sp": list(range(0, 18)), "pool": list(range(18, ndma))}
        self.dcur = {"sp": 0, "pool": 0}
        self.lastw = {}
        self.readers = {}
        self.bank = 0
        self.banktime = {}
        self.held = set()
        self.opidx = 0

    def _need(self, en, reads, writes):
        e = self.E[en]
        need = {}

        def add(dep, raw):
            src, sem, val = dep
            if src == en:
                if en in ("pe", "sp") or not raw:
                    return
                if RELAX_SAME_ENGINE_RAW and en in ("dve", "act") and val < e["cnt"]:
                    return
            if need.get(src, (None, 0))[1] < val:
                need[src] = (sem, val)

        for k in reads:
            if k in self.lastw:
                add(self.lastw[k], True)
        for k in writes:
            if k in self.lastw:
                add(self.lastw[k], False)
            for d in self.readers.get(k, {}).values():
                add(d, False)
        for src, (sem, val) in need.items():
            if e["seen"].get(src, 0) < val:
                e["seen"][src] = val
                e["prog"].append(("w", sem, val))

    def _record(self, dep, reads, writes):
        self.opidx += 1
        for k in list(reads) + list(writes):
            if isinstance(k, tuple) and k[0] == "ps":
                self.banktime[k[1]] = self.opidx
        for k in writes:
            self.lastw[k] = dep
            self.readers[k] = {}
        for k in reads:
            r = self.readers.setdefault(k, {})
            if r.get(dep[0], (None, None, 0))[2] < dep[2]:
                r[dep[0]] = dep

    def op(self, en, fn, reads=(), writes=()):
        e = self.E[en]
        self._need(en, reads, writes)
        e["cnt"] += 1
        r = Rec()
        fn(r)
        assert r.calls
        e["prog"].append(("i", r.calls))
        self._record((en, e["sem"], e["cnt"]), reads, writes)

    def dma(self, q, out, in_, reads=(), writes=(), **kw):
        e = self.E[q]
        slots = self.dpool[q]
        i = slots[self.dcur[q] % len(slots)]
        self.dcur[q] += 1
        sem = self.dsem[i]
        if self.dval[i] > 0 and e["seen"].get(("dma", i), 0) < self.dval[i]:
            e["seen"][("dma", i)] = self.dval[i]
            e["prog"].append(("w", sem, self.dval[i]))
        self._need(q, reads, writes)
        self.dval[i] += 16
        e["prog"].append(("d", out, in_, sem, kw))
        self._record((("dma", i), sem, self.dval[i]), reads, writes)

    def barrier(self, only=None, keep_pool_dma=False):
        pool_slots = set(self.dpool["pool"]) if keep_pool_dma else set()
        kept = {k: d for k, d in self.lastw.items()
                if isinstance(d[0], tuple) and d[0][0] == "dma" and d[0][1] in pool_slots}
        for en, e in self.E.items():
            if only is not None and en not in only:
                continue
            for en2, e2 in self.E.items():
                if en2 == en and en in ("pe", "sp"):
                    continue
                if e2["cnt"] > e["seen"].get(en2, 0):
                    e["seen"][en2] = e2["cnt"]
                    e["prog"].append(("w", e2["sem"], e2["cnt"]))
            for i, v in enumerate(self.dval):
                if i in pool_slots:
                    continue
                if v > e["seen"].get(("dma", i), 0):
                    e["seen"][("dma", i)] = v
                    e["prog"].append(("w", self.dsem[i], v))
        if only is None:
            self.lastw.clear()
            self.readers.clear()
            self.lastw.update(kept)

    def nextbank(self, hold=False):
        best, bi = None, None
        for b in range(8):
            if b in self.held:
                continue
            t = self.banktime.get(b, -1)
            if best is None or t < best:
                best, bi = t, b
        assert bi is not None, "no free PSUM bank"
        self.banktime[bi] = self.opidx + 0.5
        if hold:
            self.held.add(bi)
        return bi

    def release(self, b):
        self.held.discard(b)

    def emit(self):
        with self.nc.Block() as block:
            def mk(en):
                e = self.E[en]

                def f(h):
                    for it in e["prog"]:
                        if it[0] == "w":
                            h.wait_ge(it[1], it[2])
                        elif it[0] == "i":
                            ins = None
                            for (nm, a, k) in it[1]:
                                ins = getattr(h, nm)(*a, **k)
                            ins.then_inc(e["sem"], 1)
                        else:
                            h.dma_start(out=it[1], in_=it[2], **it[4]).then_inc(it[3], 16)
                return f
            block.tensor(mk("pe"))
            block.scalar(mk("act"))
            block.vector(mk("dve"))
            block.gpsimd(mk("pool"))
            block.sync(mk("sp"))


def xkeys(t0, n):
    return [("x", i) for i in range(t0 // 128, (t0 + n + 127) // 128)]


def build_nc(dbg=False):
    nc = bass.Bass("TRN2", target_bir_lowering=False)

    def din(name, shape):
        return nc.dram_tensor(name, list(shape), F32, kind="ExternalInput").ap()

    def dout(name, shape):
        return nc.dram_tensor(name, list(shape), F32, kind="ExternalOutput").ap()

    xin = din("xin", [T, D])
    scb = din("scb", [L, NQ * 30, DB])
    scc = din("scc", [L, NQ * 2, DC])
    norm1_g = din("norm1_g", [L, D])
    w_in = din("w_in", [L, D, DIN])
    a_ln_g = din("a_ln_g", [L, DA])
    a_ln_b = din("a_ln_b", [L, DA])
    a_ws = din("a_ws", [L, 8, 128, 128])
    a_bias = din("a_bias", [L, 8, 128])
    b_conv_w = din("b_conv_w", [L, 31, DB])
    b_conv_b = din("b_conv_b", [L, DB])
    b_ln_g = din("b_ln_g", [L, DB])
    b_ln_b = din("b_ln_b", [L, DB])
    c_conv_w = din("c_conv_w", [L, 3, DC])
    w_out = din("w_out", [L, D, D])
    norm2_g = din("norm2_g", [L, D])
    w_ff1 = din("w_ff1", [L, D, DFF])
    w_ff2 = din("w_ff2", [L, DFF, D])
    norm_f_g = din("norm_f_g", [D])

    y_o = dout("y", [T, D])
    ncbp_o = dout("ncb_p", [L, 30, DB])
    nccp_o = dout("ncc_p", [L, 2, DC])
    ncbs_o = dout("ncb_s", [L, NQ * 30, DB])
    nccs_o = dout("ncc_s", [L, NQ * 2, DC])
    vs_o = dout("v_s", [L, TS, DA])

    with ExitStack() as es:
        fw = FW(nc, es)
        op, dma = fw.op, fw.dma

        _uid = [0]

        def sb(stack, name, shape, dt):
            _uid[0] += 1
            return stack.enter_context(nc.sbuf_tensor("%s_%d" % (name, _uid[0]), list(shape), dt))

        xT = sb(es, "xT", [128, 8, T], F32)
        RW = sb(es, "RW", [128, 32768], BF16)
        ident = sb(es, "ident", [128, 128], F32)
        ones_bf = sb(es, "ones_bf", [128, 128], BF16)
        identb = sb(es, "identb", [128, 128], BF16)
        ones32 = sb(es, "ones32", [128, 128], F32)
        mhalf = sb(es, "mhalf", [128, 512], F32)
        cA = sb(es, "cA", [128, 128], F32)
        cB = sb(es, "cB", [128, 64], F32)
        ps = [es.enter_context(nc.psum_tensor("ps%d" % i, [128, 512], F32)) for i in range(8)]

        def bcw(l, k, c):
            j = l * 62 + k * 2 + c
            return cA[:, j:j + 1]

        def bcb(l, c):
            j = 124 + l * 2 + c
            return cA[:, j:j + 1]

        def g1(l, c):
            j = l * 8 + c
            return cB[:, j:j + 1]

        def g2(l, c):
            j = 16 + l * 8 + c
            return cB[:, j:j + 1]

        def gf(l, c):
            j = 32 + c
            return cB[:, j:j + 1]

        def blg(l, c):
            j = 40 + l * 2 + c
            return cB[:, j:j + 1]

        def blb(l, c):
            j = 44 + l * 2 + c
            return cB[:, j:j + 1]

        def ccw(l, k, c):
            j = 48 + l * 6 + k * 2 + c
            return cB[:, j:j + 1]

        def mmgroup(b, n, lhs_fn, rhs_fn, nk, reads, f32=False, off=0):
            def f(h):
                ins = None
                for k in range(nk):
                    ins = h.matmul(ps[b][:, off:off + n], lhsT=lhs_fn(k), rhs=rhs_fn(k), start=(k == 0), stop=(k == nk - 1))
                return ins
            op("pe", f, reads=reads, writes=[("ps", b)])

        Win = RW[:, 0:8 * DIN].rearrange("p (k n) -> p k n", k=8)
        Wout = RW[:, 8 * DIN:8 * DIN + 8 * D].rearrange("p (k n) -> p k n", k=8)

        WCB = [(0, 512), (512, 1024), (1024, 1536), (1536, 2304)]

        def winkeys(cb):
            return [("win", k, cb) for k in range(8)]

        def load_mixer_weights(l, part="all", first_reads=()):
            k0 = 7 if l > 0 else 0
            if part in ("all", "win"):
                for k in range(k0, 8):
                    dma("pool", Win[:, k, 1536:2304], w_in[l, k * 128:(k + 1) * 128, 1536:2304],
                        reads=(list(first_reads) if k == k0 else []), writes=[("win", k, 3)])
                for k in range(k0, 8):
                    dma("pool", Win[:, k, 0:1536], w_in[l, k * 128:(k + 1) * 128, 0:1536],
                        writes=[("win", k, 0), ("win", k, 1), ("win", k, 2)])
            if part in ("all", "wout"):
                for k in range(8):
                    dma("pool", Wout[:, k, :], w_out[l, k * 128:(k + 1) * 128, :], writes=[("wout", k)])

        op("pool", lambda h: h.memset(ident[:], 0.0), writes=["ident"])
        op("pool", lambda h: h.affine_select(out=ident[:], in_=ident[:], pattern=[[-1, 128]],
                                             compare_op=ALU.not_equal, fill=1.0, base=0, channel_multiplier=1),
           reads=["ident"], writes=["ident"])
        op("pool", lambda h: h.memset(ones_bf[:], 1.0), writes=["ones_bf"])
        op("act", lambda h: h.activation(out=identb[:], in_=ident[:], func=AF.Copy), reads=["ident"], writes=["identb"])
        op("pool", lambda h: h.memset(ones32[:], 1.0), writes=["ones32"])
        op("pool", lambda h: h.memset(mhalf[:], -0.5), writes=["mhalf"])

        with ExitStack() as s0:
            NST = 6
            stage = [sb(s0, "xst%d" % i, [128, D], F32) for i in range(NST)]
            cstA = sb(s0, "cstA", [128, 128], F32)
            cstB = sb(s0, "cstB", [64, 128], F32)
            dma("sp", cstA[0:124, :], b_conv_w.rearrange("l k (c p) -> (l k c) p", p=128), writes=["cstA"])
            dma("sp", cstA[124:128, :], b_conv_b.rearrange("l (c p) -> (l c) p", p=128), writes=["cstA2"])
            dma("sp", cstB[0:16, :], norm1_g.rearrange("l (c p) -> (l c) p", p=128), writes=["cstB0"])
            dma("sp", cstB[16:32, :], norm2_g.rearrange("l (c p) -> (l c) p", p=128), writes=["cstB1"])
            dma("sp", cstB[32:40, :], norm_f_g.rearrange("(c p) -> c p", p=128), writes=["cstB2"])
            dma("sp", cstB[40:44, :], b_ln_g.rearrange("l (c p) -> (l c) p", p=128), writes=["cstB3"])
            dma("sp", cstB[44:48, :], b_ln_b.rearrange("l (c p) -> (l c) p", p=128), writes=["cstB4"])
            dma("sp", cstB[48:60, :], c_conv_w.rearrange("l k (c p) -> (l k c) p", p=128), writes=["cstB5"])
            b = fw.nextbank()
            op("pe", lambda h, b=b: h.transpose(ps[b][:, 0:128], cstA[:, :], ident[:]),
               reads=["cstA", "cstA2", "ident"], writes=[("ps", b)])
            op("dve", lambda h, b=b: h.tensor_copy(out=cA[:], in_=ps[b][:, 0:128]), writes=[("ps", b), "cA"])
            b = fw.nextbank()
            op("pe", lambda h, b=b: h.transpose(ps[b][:, 0:60], cstB[0:60, :], ident[0:60, 0:60]),
               reads=["cstB%d" % i for i in range(6)] + ["ident"], writes=[("ps", b)])
            op("dve", lambda h, b=b: h.tensor_copy(out=cB[:, 0:60], in_=ps[b][:, 0:60]), writes=[("ps", b), "cB"])

            for i in range(T // 128):
                st = stage[i % NST]
                dma("sp", st[:], xin[i * 128:(i + 1) * 128, :], writes=[("xst", i % NST)])
                if i == 9:
                    load_mixer_weights(0, part="win", first_reads=[("xst", i % NST)])
                if i == 12:
                    o0 = 8 * DIN + 8 * D
                    dma("sp", RW[:, o0:o0 + 2048].bitcast(F32).rearrange("p (h s) -> p h s", h=8), a_ws[0].rearrange("h t s -> t h s"),
                        writes=["Wm"])
                    dma("sp", RW[:, o0 + 3584:o0 + 4608].bitcast(F32), a_ln_g[0:1, :].partition_broadcast(128), writes=["alng"])
                    dma("sp", RW[:, o0 + 4608:o0 + 5632].bitcast(F32), a_ln_b[0:1, :].partition_broadcast(128), writes=["alnb"])
                for cg in range(2):
                    b = fw.nextbank()

                    def f(h, st=st, b=b, cg=cg):
                        ins = None
                        for j in range(4):
                            c = cg * 4 + j
                            ins = h.transpose(ps[b][:, j * 128:(j + 1) * 128], st[:, c * 128:(c + 1) * 128], ident[:])
                        return ins
                    op("pe", f, reads=[("xst", i % NST), "ident"], writes=[("ps", b)])
                    o_ap = xT[:, cg * 4:(cg + 1) * 4, i * 128:(i + 1) * 128]
                    i_ap = ps[b][:].rearrange("p (c t) -> p c t", c=4)
                    if cg == 0:
                        op("act", lambda h, o=o_ap, i_=i_ap: h.activation(out=o, in_=i_, func=AF.Copy),
                           reads=[], writes=[("ps", b), ("x", i)])
                    else:
                        op("dve", lambda h, o=o_ap, i_=i_ap: h.tensor_copy(out=o, in_=i_),
                           reads=[], writes=[("ps", b), ("x", i)])
            fw.barrier(keep_pool_dma=True)

        def norm(t0, n, gfn, hout_fn, hkey, sq, tA, tB, ksq, kA, kB):
            xk = xkeys(t0, n)
            op("act", lambda h: h.activation(out=sq[:, :, :n], in_=xT[:, :, t0:t0 + n], func=AF.Square),
               reads=xk, writes=[ksq])
            b = fw.nextbank()
            mmgroup(b, n, lambda k: ones_bf[:], lambda k: sq[:, k, :n], 8, [ksq, "ones_bf"])
            op("act", lambda h: h.activation(out=tA[:, :n], in_=ps[b][:, :n], func=AF.Ln, scale=1.0 / D, bias=EPS),
               writes=[("ps", b), kA])
            op("act", lambda h: h.activation(out=tB[:, :n], in_=tA[:, :n], func=AF.Exp, scale=-0.5), reads=[kA], writes=[kB])
            for c in range(8):
                op("dve", lambda h, c=c: h.scalar_tensor_tensor(out=hout_fn(c), in0=xT[:, c, t0:t0 + n], scalar=gfn(c),
                                                               in1=tB[:, :n], op0=ALU.mult, op1=ALU.mult),
                   reads=xk + [kB, "cB"], writes=[hkey])

        for l in range(L):
            W1 = [RW[:, i * 8192:(i + 1) * 8192].rearrange("p (k n) -> p k n", k=8) for i in range(2)]

            def load_w1(blk, extra=()):
                W = W1[blk % 2]
                for k in range(8):
                    dma("pool", W[:, k, :], w_ff1[l, k * 128:(k + 1) * 128, blk * 1024:(blk + 1) * 1024],
                        writes=[("w1", blk % 2, k)] + list(extra))

            with ExitStack() as sm:
                o = 8 * DIN + 8 * D
                Wm = RW[:, o:o + 2048].bitcast(F32).rearrange("p (h s) -> p h s", h=8)
                vf = RW[:, o + 2048:o + 3072].bitcast(F32)
                hst = RW[:, o + 3072:o + 3584].bitcast(F32)
                alng = RW[:, o + 3584:o + 4608].bitcast(F32)
                alnb = RW[:, o + 4608:o + 5632].bitcast(F32)
                if l > 0:
                    load_mixer_weights(l, part="win")
                WINK = [("win", k) for k in range(8)]
                WOUTK = [("wout", k) for k in range(8)]

                NM = 256
                NPOOL = 0
                hb1 = sb(sm, "hb", [128, 8, NM], BF16)
                hb = [hb1, hb1]
                GH = sb(sm, "GH", [128, 2, 30 + NM], BF16)
                GL = sb(sm, "GL", [128, 2, 30 + NM], BF16)
                dgs = sb(sm, "dgs", [128, max(1, 2 * NPE_CONV), 128], BF16)
                sq = sb(sm, "sq", [128, 8, NM], BF16)
                uT = sb(sm, "uT", [128, 4, NM], BF16)
                cat = [sb(sm, "cat%d" % i, [128, 8, NM], BF16) for i in range(2)]
                vE = [sb(sm, "vE%d" % i, [128, DA], BF16) for i in range(2)]
                vO = [sb(sm, "vO%d" % i, [128, DA], BF16) for i in range(2)]
                gvr = [sb(sm, "gv%d" % i, [128, DA], F32) for i in range(2)]
                sq32 = None
                NT = 6
                tr = [sb(sm, "tr%d" % i, [128, NM], F32) for i in range(NT)]
                vst = [sb(sm, "vst%d" % i, [128, 16], F32) for i in range(4)]
                G = [sb(sm, "G%d" % i, [128, 2, 30 + NM], F32) for i in range(2)]
                CX = [sb(sm, "CX%d" % i, [128, 2, 2 + NM], F32) for i in range(2)]
                GS = sb(sm, "GS", [128, 2, NQ, 38], F32)
                CXS = sb(sm, "CXS", [128, 2, NQ, 10], F32)
                acc = sb(sm, "acc", [128, 2, NM], F32)
                sqb = sb(sm, "sqb", [128, 2, NM], F32)
                wsT = sb(sm, "wsT", [128, 8, 128], BF16)
                BD = sb(sm, "BD", [128, 8, 128], BF16)
                biasR = sb(sm, "biasR", [2, 512], F32)
                biasRs = sb(sm, "biasRs", [2, 512], F32)
                Ind = sb(sm, "Ind", [2, 128], F32)
                accC = sb(sm, "accC", [128, 2, NM], F32)
                Sel = gvr[1][0:8, 0:128]
                Msk = gvr[0][:, 0:128]

                state = dict(tri=0, vsi=0)
                sq32 = sq[:, :, :].rearrange("p k n -> p (k n)").bitcast(F32)
                tr_held = set()

                def T1(hold=False):
                    for _ in range(NT):
                        i = state["tri"] % NT
                        state["tri"] += 1
                        if i not in tr_held:
                            break
                    else:
                        raise AssertionError("no free temp slot")
                    if hold:
                        tr_held.add(i)
                    return tr[i], ("tr", i)

                def TR(key):
                    tr_held.discard(key[1])

                for i in range(2):
                    op("dve", lambda h, i=i: h.memset(vE[i][:], 0.0), writes=[("vE", i)])
                    op("dve", lambda h, i=i: h.memset(vO[i][:], 0.0), writes=[("vO", i)])

                def setup_dmas():
                    if l > 0:
                        dma("sp", Wm[:, :, :], a_ws[l].rearrange("h t s -> t h s"), writes=["Wm"])
                        dma("sp", alng[:], a_ln_g[l:l + 1, :].partition_broadcast(128), writes=["alng"])
                        dma("sp", alnb[:], a_ln_b[l:l + 1, :].partition_broadcast(128), writes=["alnb"])
                    dma("sp", biasR[0:2, :].rearrange("p (b t) -> p b t", b=4), a_bias[l].rearrange("(b h) t -> h b t", h=2),
                        writes=["biasR"])

                def setup_crit():
                    op("pool", lambda h: h.memset(Ind[:], 1.0), writes=["Ind"])
                    op("pool", lambda h: h.affine_select(out=Ind[:], in_=Ind[:], pattern=[[1, 128]], compare_op=ALU.is_ge, fill=0.0,
                                                         base=0, channel_multiplier=-64), reads=["Ind"], writes=["Ind"])
                    op("pool", lambda h: h.affine_select(out=Ind[:], in_=Ind[:], pattern=[[-1, 128]], compare_op=ALU.is_ge, fill=0.0,
                                                         base=63, channel_multiplier=64), reads=["Ind"], writes=["Ind"])
                    op("pool", lambda h: h.affine_select(out=Wm[:, :, :], in_=Wm[:, :, :], pattern=[[0, 8], [-1, 128]],
                                                         compare_op=ALU.is_ge, fill=0.0, base=0, channel_multiplier=1),
                       reads=["Wm"], writes=["Wm"])
                    for hg in range(2):
                        b = fw.nextbank()

                        def f(h, b=b, hg=hg):
                            ins = None
                            for j in range(4):
                                ins = h.transpose(ps[b][:, j * 128:(j + 1) * 128], Wm[:, hg * 4 + j, :], ident[:])
                            return ins
                        op("pe", f, reads=["Wm", "ident"], writes=[("ps", b)])
                        op("act", lambda h, b=b, hg=hg: h.activation(
                            out=wsT[:, hg * 4:(hg + 1) * 4, :], in_=ps[b][:].rearrange("p (c t) -> p c t", c=4), func=AF.Copy),
                            writes=[("ps", b), "wsT"])

                def setup_dgs():
                    for c in range(2):
                        for j in range(NPE_CONV):
                            k = 31 - NPE_CONV + j
                            op("act", lambda h, c=c, j=j, k=k: h.activation(out=dgs[:, c * NPE_CONV + j, :], in_=identb[:], func=AF.Copy,
                                                                            scale=bcw(l, k, c)),
                               reads=["identb", "cA"], writes=["dgs"])

                def setup_late():
                    for q in range(NQ):
                        op("dve", lambda h, q=q: h.tensor_copy(
                            out=biasRs[0:2, :].rearrange("p (b q t) -> p b q t", b=4, q=NQ)[:, :, q, :],
                            in_=biasR[0:2, :].rearrange("p (b t) -> p b t", b=4)[:, :, 0:8]),
                           reads=["biasR"], writes=["biasRs"])
                    for q in range(NQ):
                        op("dve", lambda h, q=q: h.tensor_copy(out=Sel[0:8, q * 8:(q + 1) * 8], in_=ident[0:8, 0:8]),
                           reads=["ident"], writes=[("gv", 1)])
                    op("pool", lambda h: h.memset(Msk[:], 1.0), writes=[("gv", 0)])
                    op("pool", lambda h: h.affine_select(out=Msk[:].rearrange("p (q t) -> p q t", q=NQ),
                                                         in_=Msk[:].rearrange("p (q t) -> p q t", q=NQ),
                                                         pattern=[[-8, NQ], [0, 8]], compare_op=ALU.is_ge, fill=0.0,
                                                         base=0, channel_multiplier=1),
                       reads=[("gv", 0)], writes=[("gv", 0)])
                    op("pool", lambda h: h.affine_select(out=Msk[:].rearrange("p (q t) -> p q t", q=NQ),
                                                         in_=Msk[:].rearrange("p (q t) -> p q t", q=NQ),
                                                         pattern=[[8, NQ], [0, 8]], compare_op=ALU.is_ge, fill=0.0,
                                                         base=7, channel_multiplier=-1),
                       reads=[("gv", 0)], writes=[("gv", 0)])
                    for hg in range(2):
                        b = fw.nextbank()

                        def f(h, b=b, hg=hg):
                            ins = None
                            for j in range(4):
                                ins = h.matmul(ps[b][0:8, j * 128:(j + 1) * 128], lhsT=Wm[0:8, hg * 4 + j, 0:8], rhs=Sel[0:8, :],
                                               start=True, stop=True)
                            return ins
                        op("pe", f, reads=["Wm", ("gv", 1)], writes=[("ps", b)])
                        op("act", lambda h, b=b, hg=hg: h.activation(
                            out=sq32[0:8, hg * 512:(hg + 1) * 512], in_=ps[b][0:8, :],
                            func=AF.Copy), writes=[("ps", b), "sq"])
                    for hg in range(2):
                        b = fw.nextbank()

                        def f(h, b=b, hg=hg):
                            ins = None
                            for j in range(4):
                                ins = h.matmul(ps[b][:, j * 128:(j + 1) * 128], lhsT=Sel[0:8, :], rhs=sq32[0:8, hg * 512 + j * 128:hg * 512 + (j + 1) * 128],
                                               start=True, stop=True)
                            return ins
                        op("pe", f, reads=["sq", ("gv", 1)], writes=[("ps", b)])
                        for j in range(4):
                            op("dve", lambda h, b=b, hg=hg, j=j: h.tensor_tensor(
                                out=BD[:, hg * 4 + j, :], in0=ps[b][:, j * 128:(j + 1) * 128], in1=Msk[:], op=ALU.mult),
                                reads=[("gv", 0)], writes=[("ps", b), "BD"])

                def setup_sample_history():
                    for s4 in range(4):
                        dma("sp", hst[0:120, :], scb[l, s4 * 120:(s4 + 1) * 120, :], writes=["hst"])
                        b = fw.nextbank()

                        def f(h, b=b):
                            ins = None
                            for c in range(2):
                                ins = h.transpose(ps[b][:, c * 128:c * 128 + 120], hst[0:120, c * 128:(c + 1) * 128],
                                                  ident[0:120, 0:120])
                            return ins
                        op("pe", f, reads=["hst", "ident"], writes=[("ps", b)])
                        for c in range(2):
                            op("act", lambda h, b=b, c=c, s4=s4: h.activation(
                                out=GS[:, c, s4 * 4:(s4 + 1) * 4, 0:30],
                                in_=ps[b][:, c * 128:c * 128 + 120].rearrange("p (q t) -> p q t", q=4), func=AF.Copy),
                                writes=[("ps", b), "GS"])
                    dma("sp", hst[0:32, :], scc[l, :, :], writes=["hst"])
                    b = fw.nextbank()

                    def f(h, b=b):
                        ins = None
                        for c in range(2):
                            ins = h.transpose(ps[b][:, c * 128:c * 128 + 32], hst[0:32, c * 128:(c + 1) * 128], ident[0:32, 0:32])
                        return ins
                    op("pe", f, reads=["hst", "ident"], writes=[("ps", b)])
                    for c in range(2):
                        op("act", lambda h, b=b, c=c: h.activation(
                            out=CXS[:, c, :, 0:2], in_=ps[b][:, c * 128:c * 128 + 32].rearrange("p (q t) -> p q t", q=NQ),
                            func=AF.Copy), writes=[("ps", b), "CXS"])

                op("dve", lambda h: h.memset(G[0][:, :, 0:30], 0.0), writes=[("G", 0)])
                op("dve", lambda h: h.memset(GH[:, :, 0:30], 0.0), writes=["GH"])
                op("dve", lambda h: h.memset(GL[:, :, 0:30], 0.0), writes=["GL"])
                op("dve", lambda h: h.memset(CX[0][:, :, 0:2], 0.0), writes=[("CX", 0)])

                groups = [(g * NM, NM, "p") for g in range(TP // NM)] + [(TP, TS, "s")]
                NG = len(groups)
                GI = {}

                def ginfo(gi):
                    t0, n, kind = groups[gi]
                    par = gi % 2
                    samp = kind == "s"
                    d = dict(t0=t0, n=n, samp=samp, par=par, h=hb[par], hk=("hb", 0), cat=cat[par], ck=("cat", par))
                    if samp:
                        d["a3"] = lambda ap: ap.rearrange("p (q t) -> p q t", q=NQ)
                        d["gdst"] = lambda c: GS[:, c, :, 30:38]
                        d["gsrc"] = lambda c, k: GS[:, c, :, k:k + 8]
                        d["Gk"] = "GS"
                        d["cdst"] = lambda c: CXS[:, c, :, 2:10]
                        d["csrc"] = lambda c, k: CXS[:, c, :, k:k + 8]
                        d["Ck"] = "CXS"
                    else:
                        Gc, Cc = G[par], CX[par]
                        d["a3"] = lambda ap: ap
                        d["gdst"] = lambda c, Gc=Gc, n=n: Gc[:, c, 30:30 + n]
                        d["gsrc"] = lambda c, k, Gc=Gc, n=n: Gc[:, c, k:k + n]
                        d["Gk"] = ("G", par)
                        d["cdst"] = lambda c, Cc=Cc, n=n: Cc[:, c, 2:2 + n]
                        d["csrc"] = lambda c, k, Cc=Cc, n=n: Cc[:, c, k:k + n]
                        d["Ck"] = ("CX", par)
                        d["G"], d["CX"] = Gc, Cc
                    return d

                for gi in range(NG):
                    GI[gi] = ginfo(gi)

                def st_Npre(gi):
                    d = GI[gi]
                    n, t0 = d["n"], d["t0"]
                    xk = xkeys(t0, n)
                    tA, kA = T1(hold=True)
                    op("act", lambda h: h.activation(out=sq[:, :, :n], in_=xT[:, :, t0:t0 + n], func=AF.Square),
                       reads=xk, writes=["sq"])
                    b = fw.nextbank()
                    mmgroup(b, n, lambda k: ones_bf[:], lambda k: sq[:, k, :n], 8, ["sq", "ones_bf"])
                    op("act", lambda h: h.activation(out=tA[:, :n], in_=ps[b][:, :n], func=AF.Ln, scale=1.0 / D, bias=EPS),
                       writes=[("ps", b), kA])
                    op("act", lambda h: h.activation(out=tA[:, :n], in_=tA[:, :n], func=AF.Exp, scale=-0.5), reads=[kA], writes=[kA])
                    d["nrm"] = (tA, kA)

                def st_Npost(gi):
                    d = GI[gi]
                    n, t0, hcur = d["n"], d["t0"], d["h"]
                    xk = xkeys(t0, n)
                    tA, kA = d["nrm"]
                    for c in range(8):
                        op("dve", lambda h, c=c: h.scalar_tensor_tensor(out=hcur[:, c, :n], in0=xT[:, c, t0:t0 + n], scalar=g1(l, c),
                                                                       in1=tA[:, :n], op0=ALU.mult, op1=ALU.mult),
                           reads=xk + [kA, "cB"], writes=[d["hk"]])
                    TR(kA)

                def st_P(gi, wo_prev=None):
                    d = GI[gi]
                    n, hcur, hk = d["n"], d["h"], d["hk"]

                    def proj(bk, col0, c, **kw):
                        cb = 0 if col0 < 512 else (2 if col0 < 1536 else 3)
                        mmgroup(bk, n, lambda k: Win[:, k, col0 + c * 128:col0 + (c + 1) * 128], lambda k: hcur[:, k, :n], 8,
                                winkeys(cb) + [hk], off=c * n)
                    b1 = fw.nextbank()
                    for c in range(2):
                        proj(b1, 1792, c)
                    tcps = []
                    for c in range(2):
                        tcp, tck = T1(hold=True)
                        op("act", lambda h, c=c, tcp=tcp: h.activation(out=tcp[:, :n], in_=ps[b1][:, c * n:(c + 1) * n], func=AF.Copy),
                           writes=[("ps", b1), tck])
                        tcps.append((tcp, tck))
                    b2 = fw.nextbank(hold=True)
                    for c in range(2):
                        proj(b2, 2048, c)
                    b3 = fw.nextbank(hold=True)
                    for c in range(2):
                        proj(b3, 1536, c)
                    d["pc"] = (b2, b3, tcps)
                    ba = fw.nextbank(hold=True)
                    for c in range(2):
                        proj(ba, 1024, c)
                    bg = fw.nextbank()
                    for c in range(2):
                        proj(bg, 1280, c)
                    sgs = []
                    for c in range(2):
                        sg, sgk = T1(hold=True)
                        op("act", lambda h, c=c, sg=sg: h.activation(out=sg[:, :n], in_=ps[bg][:, c * n:(c + 1) * n], func=AF.Sigmoid),
                           writes=[("ps", bg), sgk])
                        sgs.append((sg, sgk))
                    d["pb"] = (ba, sgs)
                    d["vt"] = []
                    for tt in range(n // 128):
                        b = fw.nextbank()
                        mmgroup(b, 512, lambda k, tt=tt: hcur[:, k, tt * 128:(tt + 1) * 128], lambda k: Win[:, k, 512:1024], 8,
                                winkeys(1) + [hk])
                        vi = state["vsi"] % 2
                        state["vsi"] += 1
                        gv, gk = gvr[vi], ("gv", vi)
                        op("act", lambda h, b=b, gv=gv: h.activation(out=gv[:], in_=ps[b][:], func=AF.Gelu_apprx_tanh),
                           writes=[("ps", b), gk])
                        d["vt"].append((tt, vi))
                    for j in range(2):
                        b = fw.nextbank()
                        for c in range(2):
                            proj(b, (2 * j) * 128, c)
                        op("act", lambda h, b=b, j=j: h.activation(out=uT[:, 2 * j:2 * j + 2, :n],
                                                                   in_=ps[b][:, 0:2 * n].rearrange("p (c t) -> p c t", c=2),
                                                                   func=AF.Gelu_apprx_tanh),
                           writes=[("ps", b), "uT"])
                    if wo_prev is not None:
                        st_WOpe(wo_prev)

                def st_halo(gi):
                    d = GI[gi]
                    if d["samp"] or gi == 0:
                        return
                    Cc, Cp = d["CX"], GI[gi - 1]["CX"]
                    op("act", lambda h: h.activation(out=Cc[:, :, 0:2], in_=Cp[:, :, NM:NM + 2], func=AF.Copy),
                       reads=[GI[gi - 1]["Ck"]], writes=[d["Ck"]])
                    Gc, Gp = d["G"], GI[gi - 1]["G"]
                    op("act", lambda h: h.activation(out=Gc[:, :, 0:30], in_=Gp[:, :, NM:NM + 30], func=AF.Copy),
                       reads=[GI[gi - 1]["Gk"]], writes=[d["Gk"]])
                    if NPE_CONV > 0:
                        op("act", lambda h: h.activation(out=GH[:, :, 0:30], in_=GH[:, :, NM:NM + 30], func=AF.Copy),
                           reads=["GH"], writes=["GH"])
                        op("act", lambda h: h.activation(out=GL[:, :, 0:30], in_=GL[:, :, NM:NM + 30], func=AF.Copy),
                           reads=["GL"], writes=["GL"])

                def st_PCa(gi):
                    d = GI[gi]
                    n, a3, Ck = d["n"], d["a3"], d["Ck"]
                    cdst, csrc = d["cdst"], d["csrc"]
                    b2, b3, tcps = d["pc"]
                    for c in range(2):
                        tcp, tck = tcps[c]
                        op("dve", lambda h, c=c, tcp=tcp: h.tensor_tensor(out=cdst(c), in0=a3(ps[b2][:, c * n:(c + 1) * n]),
                                                                         in1=a3(tcp[:, :n]), op=ALU.mult),
                           reads=[tck], writes=[("ps", b2), Ck])
                        TR(tck)
                    fw.release(b2)
                    for c in range(2):
                        if ENG_CONVC == "dve":
                            op("dve", lambda h, c=c: h.tensor_scalar(out=a3(accC[:, c, :n]), in0=csrc(c, 0), scalar1=ccw(l, 0, c),
                                                                     scalar2=None, op0=ALU.mult),
                               reads=[Ck, "cB"], writes=[("accC", c)])
                            for k in range(1, 3):
                                op("dve", lambda h, c=c, k=k: h.scalar_tensor_tensor(out=a3(accC[:, c, :n]), in0=csrc(c, k),
                                                                                    scalar=ccw(l, k, c), in1=a3(accC[:, c, :n]),
                                                                                    op0=ALU.mult, op1=ALU.add),
                                   reads=[Ck, "cB", ("accC", c)], writes=[("accC", c)])
                            continue
                        op("pool", lambda h, c=c: h.tensor_scalar(out=a3(accC[:, c, :n]), in0=csrc(c, 0), scalar1=ccw(l, 0, c),
                                                                  scalar2=0.0, op0=ALU.mult, op1=ALU.add),
                           reads=[Ck, "cB"], writes=[("accC", c)])
                        for k in range(1, 3):
                            ctp, ctk = T1()
                            op("pool", lambda h, c=c, k=k, ctp=ctp: h.tensor_scalar(out=a3(ctp[:, :n]), in0=csrc(c, k),
                                                                                   scalar1=ccw(l, k, c), scalar2=0.0,
                                                                                   op0=ALU.mult, op1=ALU.add),
                               reads=[Ck, "cB"], writes=[ctk])
                            op("pool", lambda h, c=c, ctp=ctp: h.tensor_tensor(out=accC[:, c, :n], in0=accC[:, c, :n], in1=ctp[:, :n],
                                                                              op=ALU.add),
                               reads=[ctk, ("accC", c)], writes=[("accC", c)])

                def st_PCb(gi):
                    d = GI[gi]
                    n, ccur, ck = d["n"], d["cat"], d["ck"]
                    b2, b3, tcps = d["pc"]
                    op("dve", lambda h: h.tensor_tensor(out=ccur[:, 6:8, :n], in0=ps[b3][:, 0:2 * n].rearrange("p (c t) -> p c t", c=2),
                                                        in1=accC[:, :, :n], op=ALU.mult),
                       reads=[("accC", 0), ("accC", 1)], writes=[("ps", b3), ck])
                    fw.release(b3)

                def st_PBdve(gi):
                    d = GI[gi]
                    n, a3, Gk, gdst = d["n"], d["a3"], d["Gk"], d["gdst"]
                    ba, sgs = d["pb"]
                    for c in range(2):
                        sg, sgk = sgs[c]
                        op("dve", lambda h, sg=sg, c=c: h.tensor_tensor(out=gdst(c), in0=a3(ps[ba][:, c * n:(c + 1) * n]),
                                                                       in1=a3(sg[:, :n]), op=ALU.mult),
                           reads=[sgk], writes=[("ps", ba), Gk])
                        TR(sgk)
                    fw.release(ba)
                    if NPE_CONV > 0 and not d["samp"]:
                        Gc = d["G"]
                        op("dve", lambda h: h.tensor_copy(out=GH[:, :, 30:30 + n], in_=Gc[:, :, 30:30 + n]),
                           reads=[Gk], writes=["GH"])
                        op("dve", lambda h: h.tensor_tensor(out=GL[:, :, 30:30 + n], in0=Gc[:, :, 30:30 + n], in1=GH[:, :, 30:30 + n],
                                                            op=ALU.subtract),
                           reads=[Gk, "GH"], writes=["GL"])

                def v4(ap):
                    return ap.rearrange("p (b h d) -> p b h d", b=4, h=2)

                def st_LNa(gi):
                    d = GI[gi]
                    d["vs"] = []
                    for (tt, vi) in d["vt"]:
                        gv, gk = gvr[vi], ("gv", vi)
                        state["vst"] = state.get("vst", 0) + 1
                        st_, sk = vst[state["vst"] % 4], ("vst", state["vst"] % 4)
                        d["vs"].append((st_, sk))
                        op("dve", lambda h, gv=gv, st_=st_: h.bn_stats(out=st_[:, 0:6], in_=gv[:]), reads=[gk], writes=[sk])
                        op("dve", lambda h, st_=st_: h.bn_aggr(out=st_[:, 6:8], in_=st_[:, 0:6]), reads=[sk], writes=[sk])
                        op("dve", lambda h, st_=st_: h.tensor_scalar(out=st_[:, 8:9], in0=st_[:, 7:8], scalar1=EPS, scalar2=None,
                                                                    op0=ALU.add), reads=[sk], writes=[sk])
                        op("pool", lambda h, st_=st_: h.tensor_tensor(out=st_[:, 9:10], in0=st_[:, 8:9], in1=mhalf[:, 0:1],
                                                                     op=ALU.pow), reads=[sk, "mhalf"], writes=[sk])

                def st_LNb(gi):
                    d = GI[gi]
                    samp = d["samp"]
                    for j, (tt, vi) in enumerate(d["vt"]):
                        gv, gk = gvr[vi], ("gv", vi)
                        st_, sk = d["vs"][j]
                        op("dve", lambda h, gv=gv, st_=st_: h.tensor_scalar(out=gv[:], in0=gv[:], scalar1=st_[:, 6:7],
                                                                           scalar2=st_[:, 9:10], op0=ALU.subtract, op1=ALU.mult),
                           reads=[gk, sk], writes=[gk])
                        op(ENG_LNAFF, lambda h, gv=gv: h.tensor_tensor(out=gv[:], in0=gv[:], in1=alng[:], op=ALU.mult),
                           reads=[gk, "alng"], writes=[gk])
                        vEc, vOc = vE[vi], vO[vi]
                        vEk, vOk = ("vE", vi), ("vO", vi)
                        if not samp:
                            op(ENG_LNAFF, lambda h, gv=gv, vEc=vEc: h.tensor_tensor(out=v4(vEc[:])[:, :, 0, :], in0=v4(gv[:])[:, :, 0, :],
                                                                                in1=v4(alnb[:])[:, :, 0, :], op=ALU.add),
                               reads=[gk, "alnb"], writes=[vEk])
                            op(ENG_LNAFF, lambda h, gv=gv, vOc=vOc: h.tensor_tensor(out=v4(vOc[:])[:, :, 1, :], in0=v4(gv[:])[:, :, 1, :],
                                                                                in1=v4(alnb[:])[:, :, 1, :], op=ALU.add),
                               reads=[gk, "alnb"], writes=[vOk])
                        else:
                            op(ENG_LNAFF, lambda h, gv=gv: h.tensor_tensor(out=vf[:], in0=gv[:], in1=alnb[:], op=ALU.add),
                               reads=[gk, "alnb"], writes=["vf"])
                            dma("sp", vs_o[l, :, :], vf[:], reads=["vf"], writes=[("o_vs", l)])
                            op("act", lambda h, vEc=vEc: h.activation(out=v4(vEc[:])[:, :, 0, :], in_=v4(vf[:])[:, :, 0, :],
                                                                     func=AF.Copy), reads=["vf"], writes=[vEk])
                            op("act", lambda h, vOc=vOc: h.activation(out=v4(vOc[:])[:, :, 1, :], in_=v4(vf[:])[:, :, 1, :],
                                                                     func=AF.Copy), reads=["vf"], writes=[vOk])

                def st_spatial(gi):
                    d = GI[gi]
                    samp, ccur, ck = d["samp"], d["cat"], d["ck"]
                    d["sp"] = []
                    for (tt, vi) in d["vt"]:
                        vEc, vOc = vE[vi], vO[vi]
                        vEk, vOk = ("vE", vi), ("vO", vi)
                        b = fw.nextbank(hold=True)
                        Wmix = BD if samp else wsT

                        def f(h, b=b, vEc=vEc, vOc=vOc, Wmix=Wmix):
                            ins = None
                            h.matmul(ps[b][:, :], lhsT=Ind[0:2, :], rhs=(biasRs if samp else biasR)[0:2, :], start=True, stop=False)
                            for blk in range(4):
                                h.matmul(ps[b][:, blk * 128:(blk + 1) * 128], lhsT=vEc[:, blk * 128:(blk + 1) * 128],
                                         rhs=Wmix[:, 2 * blk, :], start=False, stop=False)
                                ins = h.matmul(ps[b][:, blk * 128:(blk + 1) * 128], lhsT=vOc[:, blk * 128:(blk + 1) * 128],
                                               rhs=Wmix[:, 2 * blk + 1, :], start=False, stop=(blk == 3))
                            return ins
                        op("pe", f, reads=[vEk, vOk, "BD" if samp else "wsT", "Ind", "biasR", "biasRs"], writes=[("ps", b)])
                        d["sp"].append((tt, vi, b))

                def st_ya(gi):
                    d = GI[gi]
                    samp, ccur, ck = d["samp"], d["cat"], d["ck"]
                    for (tt, vi, b) in d["sp"]:
                        op("dve", lambda h, b=b, tt=tt: h.tensor_tensor(
                            out=ccur[:, 0:4, tt * 128:(tt + 1) * 128], in0=ps[b][:].rearrange("p (b t) -> p b t", b=4),
                            in1=uT[:, 0:4, tt * 128:(tt + 1) * 128], op=ALU.mult),
                            reads=["uT"], writes=[("ps", b), ck])
                        fw.release(b)

                def st_CBpool(gi):
                    d = GI[gi]
                    n, a3, Gk, gsrc = d["n"], d["a3"], d["Gk"], d["gsrc"]
                    KP = 31 - NPOOL
                    if NPOOL == 0:
                        return
                    ptp, ptk = T1()
                    for c in range(2):
                        op("pool", lambda h, c=c: h.tensor_scalar(out=a3(sqb[:, c, :n]), in0=gsrc(c, KP), scalar1=bcw(l, KP, c),
                                                                  scalar2=0.0, op0=ALU.mult, op1=ALU.add),
                           reads=[Gk, "cA"], writes=[("sqb", c)])
                        for k in range(KP + 1, 31):
                            op("pool", lambda h, c=c, k=k: h.tensor_scalar(out=a3(ptp[:, :n]), in0=gsrc(c, k), scalar1=bcw(l, k, c),
                                                                           scalar2=0.0, op0=ALU.mult, op1=ALU.add),
                               reads=[Gk, "cA"], writes=[ptk])
                            op("pool", lambda h, c=c: h.tensor_tensor(out=sqb[:, c, :n], in0=sqb[:, c, :n], in1=ptp[:, :n], op=ALU.add),
                               reads=[ptk, ("sqb", c)], writes=[("sqb", c)])

                def conv_split(gi):
                    d = GI[gi]
                    KD = 31 - NPOOL
                    if NPE_CONV > 0 and not d["samp"]:
                        KD = 31 - NPE_CONV
                    return KD, max(1, (KD * 2) // 3)

                def st_CBpe(gi):
                    d = GI[gi]
                    if NPE_CONV == 0 or d["samp"]:
                        return
                    n = d["n"]
                    bcv = fw.nextbank(hold=True)
                    d["cvb"] = bcv
                    taps = list(range(31 - NPE_CONV, 31))
                    for c in range(2):
                        def f(h, c=c):
                            ins = None
                            for j, k in enumerate(taps):
                                dg = dgs[:, c * NPE_CONV + j, :]
                                h.matmul(ps[bcv][:, c * n:(c + 1) * n], lhsT=dg, rhs=GH[:, c, k:k + n], start=(j == 0), stop=False)
                                ins = h.matmul(ps[bcv][:, c * n:(c + 1) * n], lhsT=dg, rhs=GL[:, c, k:k + n], start=False,
                                               stop=(j == len(taps) - 1))
                            return ins
                        op("pe", f, reads=["dgs", "GH", "GL"], writes=[("ps", bcv)])

                def st_CBinit(gi):
                    d = GI[gi]
                    n, a3, Gk, gsrc = d["n"], d["a3"], d["Gk"], d["gsrc"]
                    for c in range(2):
                        op("dve", lambda h, c=c: h.tensor_scalar(out=a3(acc[:, c, :n]), in0=gsrc(c, 0), scalar1=bcw(l, 0, c),
                                                                 scalar2=bcb(l, c), op0=ALU.mult, op1=ALU.add),
                           reads=[Gk, "cA"], writes=[("acc", c)])
                    if "cvb" in d:
                        bcv = d["cvb"]
                        op("dve", lambda h: h.tensor_tensor(out=acc[:, :, :n], in0=ps[bcv][:, 0:2 * n].rearrange("p (c t) -> p c t", c=2),
                                                            in1=acc[:, :, :n], op=ALU.add),
                           reads=[("acc", 0), ("acc", 1)], writes=[("ps", bcv), ("acc", 0), ("acc", 1)])
                        fw.release(bcv)

                def st_CBdve(gi, part, merge):
                    d = GI[gi]
                    n, a3, Gk, gsrc = d["n"], d["a3"], d["Gk"], d["gsrc"]
                    KD, K1 = conv_split(gi)
                    ks = range(1, K1) if part == 0 else range(K1, KD)
                    for k in ks:
                        for c in range(2):
                            op("dve", lambda h, k=k, c=c: h.scalar_tensor_tensor(out=a3(acc[:, c, :n]), in0=gsrc(c, k),
                                                                                scalar=bcw(l, k, c), in1=a3(acc[:, c, :n]),
                                                                                op0=ALU.mult, op1=ALU.add),
                               reads=[Gk, "cA", ("acc", c)], writes=[("acc", c)])

                ACCK = [("acc", 0), ("acc", 1)]

                def st_LB1(gi):
                    d = GI[gi]
                    n = d["n"]
                    op("act", lambda h: h.activation(out=sqb[:, :, :n], in_=acc[:, :, :n], func=AF.Square),
                       reads=ACCK, writes=[("sqb", 0), ("sqb", 1)])
                    b1 = fw.nextbank()
                    mmgroup(b1, n, lambda k: ones32[:], lambda k: acc[:, k, :n], 2, ACCK + ["ones32"])
                    b2 = fw.nextbank()
                    mmgroup(b2, n, lambda k: ones32[:], lambda k: sqb[:, k, :n], 2, [("sqb", 0), ("sqb", 1), "ones32"])
                    mean, mk_ = T1(hold=True)
                    ex2, ek_ = T1(hold=True)
                    op("act", lambda h: h.activation(out=mean[:, :n], in_=ps[b1][:, :n], func=AF.Identity, scale=1.0 / DB),
                       writes=[("ps", b1), mk_])
                    op("act", lambda h: h.activation(out=ex2[:, :n], in_=ps[b2][:, :n], func=AF.Identity, scale=1.0 / DB, bias=EPS),
                       writes=[("ps", b2), ek_])
                    d["lb"] = (mean, mk_, ex2, ek_)

                def st_LB2a(gi):
                    d = GI[gi]
                    n = d["n"]
                    mean, mk_, ex2, ek_ = d["lb"]
                    m2, m2k = T1(hold=True)
                    op("dve", lambda h: h.tensor_tensor(out=m2[:, :n], in0=mean[:, :n], in1=mean[:, :n], op=ALU.mult),
                       reads=[mk_], writes=[m2k])
                    op("dve", lambda h: h.tensor_tensor(out=m2[:, :n], in0=ex2[:, :n], in1=m2[:, :n], op=ALU.subtract),
                       reads=[ek_, m2k], writes=[m2k])
                    op("act", lambda h: h.activation(out=m2[:, :n], in_=m2[:, :n], func=AF.Ln), reads=[m2k], writes=[m2k])
                    op("act", lambda h: h.activation(out=m2[:, :n], in_=m2[:, :n], func=AF.Exp, scale=-0.5), reads=[m2k], writes=[m2k])
                    d["lb2"] = (m2, m2k)

                def st_LB2b(gi):
                    d = GI[gi]
                    n, ccur, ck = d["n"], d["cat"], d["ck"]
                    mean, mk_, ex2, ek_ = d["lb"]
                    m2, m2k = d["lb2"]
                    rsb, rk_ = m2, m2k
                    for c in range(2):
                        op("dve", lambda h, c=c: h.tensor_tensor(out=acc[:, c, :n], in0=acc[:, c, :n], in1=mean[:, :n],
                                                                 op=ALU.subtract), reads=[("acc", c), mk_], writes=[("acc", c)])
                        op("dve", lambda h, c=c: h.tensor_tensor(out=acc[:, c, :n], in0=acc[:, c, :n], in1=rsb[:, :n],
                                                                 op=ALU.mult), reads=[("acc", c), rk_], writes=[("acc", c)])
                        op("act", lambda h, c=c: h.activation(out=ccur[:, 4 + c, :n], in_=acc[:, c, :n], func=AF.Silu,
                                                              scale=blg(l, c), bias=blb(l, c)),
                           reads=[("acc", c), "cB"], writes=[ck])
                    TR(mk_)
                    TR(ek_)
                    TR(m2k)

                def st_WOpe(gi):
                    d = GI[gi]
                    n, ccur, ck = d["n"], d["cat"], d["ck"]
                    d["wo"] = {}
                    for j in range(4):
                        b = fw.nextbank(hold=True)
                        for c in range(2):
                            m = 2 * j + c
                            mmgroup(b, n, lambda k, m=m: Wout[:, k, m * 128:(m + 1) * 128], lambda k: ccur[:, k, :n], 8,
                                    WOUTK + [ck], off=c * n)
                        d["wo"][j] = b

                def st_WOdve(gi):
                    d = GI[gi]
                    n, t0 = d["n"], d["t0"]
                    xk = xkeys(t0, n)
                    for j in range(4):
                        b = d["wo"][j]
                        op("dve", lambda h, b=b, j=j: h.tensor_tensor(out=xT[:, 2 * j:2 * j + 2, t0:t0 + n],
                                                                     in0=ps[b][:, 0:2 * n].rearrange("p (c t) -> p c t", c=2),
                                                                     in1=xT[:, 2 * j:2 * j + 2, t0:t0 + n], op=ALU.add),
                           reads=xk, writes=[("ps", b)] + xk)
                        fw.release(b)

                def st_state(gi):
                    d = GI[gi]
                    if gi == TP // NM - 1:
                        Gc, Cc, Gk, Ck = d["G"], d["CX"], d["Gk"], d["Ck"]
                        so, sok = T1()
                        b = fw.nextbank()

                        def f(h):
                            ins = None
                            for c in range(2):
                                ins = h.transpose(ps[b][0:30, c * 128:(c + 1) * 128], Gc[:, c, NM:NM + 30], ident[:])
                            return ins
                        op("pe", f, reads=[Gk, "ident"], writes=[("ps", b)])
                        op("act", lambda h: h.activation(out=so[0:30, :], in_=ps[b][0:30, 0:256], func=AF.Copy),
                           writes=[("ps", b), sok])
                        dma("sp", ncbp_o[l, :, :], so[0:30, :], reads=[sok], writes=[("o_ncbp", l)])
                        so2, sok2 = T1()
                        b2 = fw.nextbank()

                        def f(h):
                            ins = None
                            for c in range(2):
                                ins = h.transpose(ps[b2][0:2, c * 128:(c + 1) * 128], Cc[:, c, NM:NM + 2], ident[:])
                            return ins
                        op("pe", f, reads=[Ck, "ident"], writes=[("ps", b2)])
                        op("act", lambda h: h.activation(out=so2[0:2, :], in_=ps[b2][0:2, 0:256], func=AF.Copy),
                           writes=[("ps", b2), sok2])
                        dma("sp", nccp_o[l, :, :], so2[0:2, :], reads=[sok2], writes=[("o_nccp", l)])
                    if d["samp"]:
                        for c in range(2):
                            op("act", lambda h, c=c: h.activation(out=sq32[:, c * 480:(c + 1) * 480].rearrange("p (q t) -> p q t", q=NQ),
                                                                  in_=GS[:, c, :, 8:38], func=AF.Copy),
                               reads=["GS"], writes=["sq"])
                        op("act", lambda h: h.activation(out=gvr[0][:, 0:64].rearrange("p (c q t) -> p c q t", c=2, q=NQ),
                                                         in_=CXS[:, :, :, 8:10], func=AF.Copy),
                           reads=["CXS"], writes=[("gv", 0)])
                        for s4 in range(4):
                            b = fw.nextbank()
                            so, sok = T1()

                            def f(h, b=b, s4=s4):
                                ins = None
                                for c in range(2):
                                    ins = h.transpose(ps[b][0:120, c * 128:(c + 1) * 128], sq32[:, c * 480 + s4 * 120:c * 480 + (s4 + 1) * 120], ident[:])
                                return ins
                            op("pe", f, reads=["sq", "ident"], writes=[("ps", b)])
                            op("act", lambda h, b=b, so=so: h.activation(out=so[0:120, :], in_=ps[b][0:120, 0:256], func=AF.Copy),
                               writes=[("ps", b), sok])
                            dma("sp", ncbs_o[l, s4 * 120:(s4 + 1) * 120, :], so[0:120, :], reads=[sok],
                                writes=[("o_ncbs", l, s4)])
                        b = fw.nextbank()
                        so, sok = T1()

                        def f(h, b=b):
                            ins = None
                            for c in range(2):
                                ins = h.transpose(ps[b][0:32, c * 128:(c + 1) * 128], gvr[0][:, c * 32:(c + 1) * 32], ident[:])
                            return ins
                        op("pe", f, reads=[("gv", 0), "ident"], writes=[("ps", b)])
                        op("act", lambda h, b=b, so=so: h.activation(out=so[0:32, :], in_=ps[b][0:32, 0:256], func=AF.Copy),
                           writes=[("ps", b), sok])
                        dma("sp", nccs_o[l, :, :], so[0:32, :], reads=[sok], writes=[("o_nccs", l)])

                setup_dmas()
                load_mixer_weights(l, part="wout")
                st_Npre(0)
                st_Npost(0)
                st_P(0)
                st_PCa(0)
                st_PBdve(0)
                st_LNa(0)
                st_LNb(0)
                setup_crit()
                st_spatial(0)
                st_PCb(0)
                st_ya(0)
                setup_dgs()
                st_Npre(1)
                st_Npost(1)
                st_CBpe(0)
                setup_late()
                for gi in range(NG):
                    nx = gi + 1 if gi + 1 < NG else None
                    n2 = gi + 2 if gi + 2 < NG else None
                    if gi == 2:
                        setup_sample_history()
                    st_CBinit(gi)
                    if nx is not None:
                        st_halo(nx)
                        st_P(nx, wo_prev=(gi - 1 if gi > 0 else None))
                    elif gi > 0:
                        st_WOpe(gi - 1)
                    if n2 is not None:
                        st_Npre(n2)
                    st_CBdve(gi, 0, False)
                    if nx is not None:
                        st_PCa(nx)
                        st_PBdve(nx)
                        st_LNa(nx)
                    st_CBdve(gi, 1, True)
                    if gi > 0:
                        st_WOdve(gi - 1)
                        st_state(gi - 1)
                    st_LB1(gi)
                    if nx is not None:
                        st_LNb(nx)
                        st_CBpe(nx)
                        st_spatial(nx)
                        st_PCb(nx)
                    if n2 is not None:
                        st_Npost(n2)
                    st_LB2a(gi)
                    if nx is not None:
                        st_ya(nx)
                    st_LB2b(gi)
                load_w1(0, extra=[("win", kk, cb) for kk in range(8) for cb in range(4)])
                ff_w1_0_prefetched = True
                st_WOpe(NG - 1)
                st_WOdve(NG - 1)
                st_state(NG - 1)
                fw.barrier()

            with ExitStack() as sf:
                h2T = sb(sf, "h2T", [128, 8, T], BF16)
                ablk = sb(sf, "ablk", [128, 8, T], BF16)
                W2 = RW[:, 16384:24576].rearrange("p (k n) -> p k n", k=8)
                o = 24576
                sq2 = RW[:, o:o + 4096].rearrange("p (k n) -> p k n", k=8)
                tA2 = RW[:, o + 4096:o + 5120].bitcast(F32)
                tB2 = RW[:, o + 5120:o + 6144].bitcast(F32)
                rl = [RW[:, o + 6144 + i * 512:o + 6144 + (i + 1) * 512] for i in range(4)]
                fgroups = [(0, 436), (436, 435), (871, 435), (1306, 435), (1741, 435)]

                def load_w2(blk):
                    for k in range(8):
                        dma("pool", W2[:, k, :], w_ff2[l, blk * 1024 + k * 128:blk * 1024 + (k + 1) * 128, :],
                            writes=[("w2", k)])

                load_w2(0)
                load_w1(1)
                def ffnorm(gi):
                    t0, n = fgroups[gi]
                    norm(t0, n, lambda c: g2(l, c), lambda c, t0=t0, n=n: h2T[:, c, t0:t0 + n], ("h2", gi), sq2, tA2, tB2, "sq2", "tA2", "tB2")
                ffnorm(0)
                ffnorm(1)
                rli = 0
                for blk in range(4):
                    W = W1[blk % 2]
                    W1K = [("w1", blk % 2, k) for k in range(8)]
                    W2K = [("w2", k) for k in range(8)]
                    for gi, (t0, n) in enumerate(fgroups):
                        for hc in range(8):
                            b = fw.nextbank()
                            mmgroup(b, n, lambda k, hc=hc, W=W: W[:, k, hc * 128:(hc + 1) * 128],
                                    lambda k, t0=t0, n=n: h2T[:, k, t0:t0 + n], 8, W1K + [("h2", gi)])
                            r = rl[rli % 4]
                            rk = ("rl", rli % 4)
                            rli += 1
                            op("act", lambda h, b=b, r=r, n=n: h.activation(out=r[:, :n], in_=ps[b][:, :n], func=AF.Relu),
                               writes=[("ps", b), rk])
                            op("pool", lambda h, r=r, hc=hc, t0=t0, n=n: h.tensor_tensor(out=ablk[:, hc, t0:t0 + n], in0=r[:, :n],
                                                                                       in1=r[:, :n], op=ALU.mult),
                               reads=[rk], writes=[("a", gi)])
                        if blk == 0 and gi + 2 < len(fgroups):
                            ffnorm(gi + 2)
                    if blk == 3 and l + 1 < L:
                        allw1 = [("w1", pp, kk) for pp in range(2) for kk in range(8)]
                        for k in range(7):
                            dma("pool", Win[:, k, :], w_in[l + 1, k * 128:(k + 1) * 128, :], writes=[("win", k, cb) for cb in range(4)] + allw1)
                    for gi, (t0, n) in enumerate(fgroups):
                        xk = xkeys(t0, n)
                        for m in range(8):
                            b = fw.nextbank()
                            mmgroup(b, n, lambda k, m=m: W2[:, k, m * 128:(m + 1) * 128],
                                    lambda k, t0=t0, n=n: ablk[:, k, t0:t0 + n], 8, W2K + [("a", gi)])
                            op("dve", lambda h, b=b, m=m, t0=t0, n=n: h.tensor_tensor(out=xT[:, m, t0:t0 + n], in0=ps[b][:, :n],
                                                                                     in1=xT[:, m, t0:t0 + n], op=ALU.add),
                               reads=xk, writes=[("ps", b)] + xk)
                    if blk + 1 < 4:
                        load_w2(blk + 1)
                    if blk + 2 < 4:
                        load_w1(blk + 2)
                fw.barrier(keep_pool_dma=True)

        with ExitStack() as so:
            sq3 = sb(so, "sq3", [128, 8, 512], BF16)
            tA3 = sb(so, "tA3", [128, 512], F32)
            tB3 = sb(so, "tB3", [128, 512], F32)
            yT = [sb(so, "yT%d" % i, [128, 8, 512], F32) for i in range(2)]
            ost = [sb(so, "ost%d" % i, [128, D], F32) for i in range(2)]
            fgroups = [(g * 512, 512) for g in range(4)] + [(TP, TS)]
            ost = ost + [sb(so, "ostx%d" % i, [128, D], F32) for i in range(2)]
            NOST = len(ost)
            oi = [0]

            def fin_pre(gi):
                t0, n = fgroups[gi]
                xk = xkeys(t0, n)
                op("act", lambda h: h.activation(out=sq3[:, :, :n], in_=xT[:, :, t0:t0 + n], func=AF.Square),
                   reads=xk, writes=["sq3"])
                b = fw.nextbank()
                mmgroup(b, n, lambda k: ones_bf[:], lambda k: sq3[:, k, :n], 8, ["sq3", "ones_bf"])
                op("act", lambda h: h.activation(out=tA3[:, :n], in_=ps[b][:, :n], func=AF.Ln, scale=1.0 / D, bias=EPS),
                   writes=[("ps", b), "tA3"])

            def fin_post(gi):
                t0, n = fgroups[gi]
                xk = xkeys(t0, n)
                yc = yT[gi % 2]
                op("act", lambda h: h.activation(out=tB3[:, :n], in_=tA3[:, :n], func=AF.Exp, scale=-0.5), reads=["tA3"], writes=["tB3"])
                for c in range(8):
                    op("dve", lambda h, c=c: h.scalar_tensor_tensor(out=yc[:, c, :n], in0=xT[:, c, t0:t0 + n], scalar=gf(0, c),
                                                                   in1=tB3[:, :n], op0=ALU.mult, op1=ALU.mult),
                       reads=xk + ["tB3", "cB"], writes=[("yT", gi % 2)])

            def fin_out(gi):
                t0, n = fgroups[gi]
                yc = yT[gi % 2]
                for tt in range(n // 128):
                    os_ = ost[oi[0] % NOST]
                    ok = ("ost", oi[0] % NOST)
                    oi[0] += 1
                    for cg in range(2):
                        b = fw.nextbank()

                        def f(h, b=b, cg=cg, tt=tt):
                            ins = None
                            for j in range(4):
                                ins = h.transpose(ps[b][:, j * 128:(j + 1) * 128], yc[:, cg * 4 + j, tt * 128:(tt + 1) * 128], ident[:])
                            return ins
                        op("pe", f, reads=[("yT", gi % 2), "ident"], writes=[("ps", b)])
                        if cg == 0:
                            op("act", lambda h, b=b, os_=os_: h.activation(out=os_[:, 0:512], in_=ps[b][:], func=AF.Copy),
                               writes=[("ps", b), ok])
                        else:
                            op("dve", lambda h, b=b, os_=os_: h.tensor_copy(out=os_[:, 512:1024], in_=ps[b][:]),
                               writes=[("ps", b), ok])
                    r0 = t0 + tt * 128
                    dma("sp", y_o[r0:r0 + 128, :], os_[:], reads=[ok], writes=[("o_y", r0)])

            fin_pre(0)
            fin_post(0)
            for gi in range(len(fgroups)):
                if gi + 1 < len(fgroups):
                    fin_pre(gi + 1)
                    fin_post(gi + 1)
                fin_out(gi)
            fw.barrier(only=["sp"])
        fw.emit()
    return nc


_NC_CACHE = {}


def kernel(x_prompt, x_sample, state_conv_b, state_conv_c, norm1_g, w_in, a_ln_g, a_ln_b, a_ws, a_bias,
           b_conv_w, b_conv_b, b_ln_g, b_ln_b, c_conv_w, w_out, norm2_g, w_ff1, w_ff2, norm_f_g):
    f32 = lambda a: np.ascontiguousarray(np.asarray(a, dtype=np.float32))
    x_prompt, x_sample = f32(x_prompt), f32(x_sample)
    state_conv_b, state_conv_c = f32(state_conv_b), f32(state_conv_c)
    shared = dict(norm1_g=f32(norm1_g), w_in=f32(w_in), a_ln_g=f32(a_ln_g), a_ln_b=f32(a_ln_b), a_ws=f32(a_ws),
                  a_bias=f32(a_bias), b_conv_w=f32(b_conv_w), b_conv_b=f32(b_conv_b), b_ln_g=f32(b_ln_g),
                  b_ln_b=f32(b_ln_b), c_conv_w=f32(c_conv_w), w_out=f32(w_out), norm2_g=f32(norm2_g),
                  w_ff1=f32(w_ff1), w_ff2=f32(w_ff2), norm_f_g=f32(norm_f_g))
    in_maps = []
    for c in range(NCORES):
        xs = x_sample[c * NQ:(c + 1) * NQ].reshape(TS, D)
        m = dict(shared)
        m["xin"] = np.ascontiguousarray(np.concatenate([x_prompt[c], xs], axis=0))
        m["scb"] = np.ascontiguousarray(state_conv_b[:, c * NQ:(c + 1) * NQ].reshape(L, NQ * 30, DB))
        m["scc"] = np.ascontiguousarray(state_conv_c[:, c * NQ:(c + 1) * NQ].reshape(L, NQ * 2, DC))
        in_maps.append(m)
    if "nc" not in _NC_CACHE:
        _NC_CACHE["nc"] = build_nc()
    nc = _NC_CACHE["nc"]
    res = run_bass_kernel_spmd(nc, in_maps, core_ids=list(range(NCORES)))
    R = res.results
    y_prompt = np.stack([R[c]["y"][:TP] for c in range(NCORES)], axis=0)
    y_sample = np.concatenate([R[c]["y"][TP:].reshape(NQ, 8, D) for c in range(NCORES)], axis=0)
    ncb_p = np.stack([R[c]["ncb_p"] for c in range(NCORES)], axis=1)
    ncc_p = np.stack([R[c]["ncc_p"] for c in range(NCORES)], axis=1)
    ncb_s = np.concatenate([R[c]["ncb_s"].reshape(L, NQ, 30, DB) for c in range(NCORES)], axis=1)
    ncc_s = np.concatenate([R[c]["ncc_s"].reshape(L, NQ, 2, DC) for c in range(NCORES)], axis=1)
    v_s = np.concatenate([R[c]["v_s"].reshape(L, NQ, 8, DA) for c in range(NCORES)], axis=1)
    return (y_prompt.astype(np.float32), y_sample.astype(np.float32), ncb_p.astype(np.float32), ncc_p.astype(np.float32),
            ncb_s.astype(np.float32), ncc_s.astype(np.float32), v_s.astype(np.float32))
```
